# Optimizing a Trainium2 kernel written in Bass

```python
import math
import jax, jax.numpy as jnp
from jax import lax
import numpy as np


D_MODEL = 2048
BATCH = 8
SEQ = 4096
DEPTH = 1

SSM_WIDTH = D_MODEL // 2
SSM_GROUP = 16
SSM_N_GROUPS = SSM_WIDTH // SSM_GROUP
SSM_STATE = 64
ATTN_HEAD_DIM = 64
ATTN_Q_HEADS = (D_MODEL - SSM_WIDTH) // ATTN_HEAD_DIM
ATTN_KV_HEADS = 4
ATTN_REP = ATTN_Q_HEADS // ATTN_KV_HEADS
ATTN_Q_WIDTH = ATTN_Q_HEADS * ATTN_HEAD_DIM
ATTN_KV_WIDTH = ATTN_KV_HEADS * ATTN_HEAD_DIM
WINDOW = 128
MIX_WIDTH = SSM_WIDTH + ATTN_Q_WIDTH
IN_PROJ_WIDTH = SSM_WIDTH + ATTN_Q_WIDTH + 2 * ATTN_KV_WIDTH
REL_BUCKETS = 32
REL_MAX_DIST = 128
MEM_LEN = 256
MEM_HEADS = 4
MEM_HEAD_DIM = 128
MEM_WIDTH = MEM_HEADS * MEM_HEAD_DIM
PEER_HEADS = 8
PEER_N_KEYS = 128
PEER_N_EXPERTS = PEER_N_KEYS * PEER_N_KEYS
PEER_TOPK = 16
PEER_KEY_DIM = 256
PEER_HALF = PEER_KEY_DIM // 2
PEER_TOKEN_BLOCK = 128
NORM_EPS = 1e-6
DT_MIN = 1e-3
DT_MAX = 1e-1

kernel_name = 'hybrid_s5_swa_sink_peer_block'

F32 = jnp.float32


def rmsnorm(x, g):
    x32 = x.astype(F32)
    y = x32 * lax.rsqrt(jnp.mean(x32 * x32, axis=-1, keepdims=True) + NORM_EPS)
    return (y * g.astype(F32)).astype(x.dtype)


def s5_mixer(u, lam_re, lam_im, b_re, b_im, c_re, c_im, d, log_dt, w_glu, b_glu):
    bsz, L, _ = u.shape
    lam = lax.complex(lam_re.astype(F32), lam_im.astype(F32))
    dt = jnp.exp(log_dt.astype(F32))[:, None]
    lam_bar = jnp.exp(lam * dt)
    b = lax.complex(b_re.astype(F32), b_im.astype(F32))
    b_bar = ((lam_bar - 1.0) / lam)[..., None] * b
    c = lax.complex(c_re.astype(F32), c_im.astype(F32))
    ug = u.astype(F32).reshape(bsz, L, SSM_N_GROUPS, SSM_GROUP)

    def combine(left, right):
        a_l, s_l = left
        a_r, s_r = right
        return a_r * a_l, a_r * s_l + s_r

    def scan_one(u_seq):
        bu = jnp.einsum('gph,lgh->lgp', b_bar, u_seq.astype(jnp.complex64))
        a = jnp.broadcast_to(lam_bar, bu.shape)
        _, states = lax.associative_scan(combine, (a, bu), axis=0)
        return jnp.einsum('ghp,lgp->lgh', c, states).real

    y = lax.map(scan_one, ug) + d.astype(F32)[None, None] * ug
    y = jax.nn.gelu(y.reshape(bsz, L, SSM_WIDTH))
    return y * jax.nn.sigmoid(y @ w_glu.astype(F32) + b_glu.astype(F32))


def t5_bucket(dist):
    max_exact = REL_BUCKETS // 2
    d_f = jnp.maximum(dist, 1).astype(F32)
    large = max_exact + (jnp.log(d_f / max_exact) / math.log(REL_MAX_DIST / max_exact)
                         * (REL_BUCKETS - max_exact)).astype(jnp.int32)
    large = jnp.minimum(large, REL_BUCKETS - 1)
    return jnp.where(dist < max_exact, dist, large)


def sliding_window_gqa_sinks(q, k, v, sinks, rel_bias):
    bsz, L = q.shape[0], q.shape[1]
    nb = L // WINDOW
    qb = q.reshape(bsz, nb, WINDOW, ATTN_KV_HEADS, ATTN_REP, ATTN_HEAD_DIM)

    def windows(t):
        tp = jnp.pad(t, ((0, 0), (WINDOW, 0), (0, 0), (0, 0)))
        tp = tp.reshape(bsz, nb + 1, WINDOW, ATTN_KV_HEADS, ATTN_HEAD_DIM)
        return jnp.concatenate([tp[:, :-1], tp[:, 1:]], axis=2)

    kw = windows(k)
    vw = windows(v)
    scores = jnp.einsum('bnqgrd,bnkgd->bngrqk', qb, kw).astype(F32) * (ATTN_HEAD_DIM ** -0.5)
    qi = jnp.arange(WINDOW)[:, None]
    kj = jnp.arange(2 * WINDOW)[None, :]
    dist = qi + WINDOW - kj
    in_window = (dist >= 0) & (dist < WINDOW)
    key_pos = jnp.arange(nb)[:, None, None] * WINDOW + kj[None] - WINDOW
    valid = in_window[None] & (key_pos >= 0)
    bias = rel_bias.astype(F32)[t5_bucket(jnp.clip(dist, 0, WINDOW - 1))]
    bias = bias.transpose(2, 0, 1).reshape(ATTN_KV_HEADS, ATTN_REP, WINDOW, 2 * WINDOW)
    scores = jnp.where(valid[None, :, None, None], scores + bias[None, None], -jnp.inf)
    sink = jnp.broadcast_to(
        sinks.astype(F32).reshape(ATTN_KV_HEADS, ATTN_REP)[None, None, :, :, None, None],
        scores.shape[:-1] + (1,))
    p = jax.nn.softmax(jnp.concatenate([scores, sink], axis=-1), axis=-1)[..., :-1]
    out = jnp.einsum('bngrqk,bnkgd->bnqgrd', p.astype(vw.dtype), vw)
    return out.reshape(bsz, L, ATTN_Q_WIDTH)


def memory_cross_attention(hn, mem_n, w_cq, w_ckv, w_co):
    bsz, L, _ = hn.shape
    q = (hn @ w_cq).reshape(bsz, L, MEM_HEADS, MEM_HEAD_DIM)
    kv = (mem_n @ w_ckv).reshape(bsz, mem_n.shape[1], 2, MEM_HEADS, MEM_HEAD_DIM)
    k = kv[:, :, 0]
    v = kv[:, :, 1]
    s = jnp.einsum('blhd,bmhd->bhlm', q, k).astype(F32) * (MEM_HEAD_DIM ** -0.5)
    p = jax.nn.softmax(s, axis=-1)
    o = jnp.einsum('bhlm,bmhd->blhd', p.astype(v.dtype), v).reshape(bsz, L, MEM_WIDTH)
    return o @ w_co


def peer_ffn(hn, w_q, sub_keys, u_tab, v_tab):
    bsz, L, D = hn.shape
    tokens = hn.reshape(-1, PEER_TOKEN_BLOCK, D)

    def block(xb):
        T = xb.shape[0]
        q = (xb @ w_q).reshape(T, PEER_HEADS, 2, PEER_HALF)
        s = jnp.einsum('thcd,hckd->thck', q, sub_keys).astype(F32)
        s_top, i_top = lax.top_k(s, PEER_TOPK)
        cand = s_top[:, :, 0, :, None] + s_top[:, :, 1, None, :]
        cand_idx = i_top[:, :, 0, :, None] * PEER_N_KEYS + i_top[:, :, 1, None, :]
        best, pos = lax.top_k(cand.reshape(T, PEER_HEADS, PEER_TOPK * PEER_TOPK), PEER_TOPK)
        idx = jnp.take_along_axis(cand_idx.reshape(T, PEER_HEADS, PEER_TOPK * PEER_TOPK), pos, axis=-1)
        g = jax.nn.softmax(best, axis=-1)
        act = jax.nn.gelu(jnp.einsum('td,thkd->thk', xb, u_tab[idx]).astype(F32))
        coef = (g * act).astype(xb.dtype)
        return jnp.einsum('thk,thkd->td', coef, v_tab[idx])

    return lax.map(block, tokens).reshape(bsz, L, D)


def setup_inputs(seed: int = 0) -> dict:
    key = jax.random.key(seed)
    ks = jax.random.split(key, 32)
    nrm = lambda k, shape, s: jax.random.normal(k, shape, F32) * s
    G, P, H = SSM_N_GROUPS, SSM_STATE, SSM_GROUP
    x = jax.random.normal(ks[0], (BATCH, SEQ, D_MODEL), F32)
    mem = jax.random.normal(ks[1], (BATCH, MEM_LEN, D_MODEL), F32)
    rel_bias = nrm(ks[2], (REL_BUCKETS, ATTN_Q_HEADS), 0.5)
    norm_mix = 1.0 + nrm(ks[3], (DEPTH, D_MODEL), 0.02)
    w_in = nrm(ks[4], (DEPTH, D_MODEL, IN_PROJ_WIDTH), D_MODEL ** -0.5)
    n_idx = jnp.arange(P, dtype=F32)
    ssm_lambda_re = -0.5 + nrm(ks[5], (DEPTH, G, P), 0.01)
    ssm_lambda_im = math.pi * n_idx[None, None] + nrm(ks[6], (DEPTH, G, P), 0.01)
    ssm_b_re = nrm(ks[7], (DEPTH, G, P, H), (2 * H) ** -0.5)
    ssm_b_im = nrm(ks[8], (DEPTH, G, P, H), (2 * H) ** -0.5)
    ssm_c_re = nrm(ks[9], (DEPTH, G, H, P), (2 * P) ** -0.5)
    ssm_c_im = nrm(ks[10], (DEPTH, G, H, P), (2 * P) ** -0.5)
    ssm_d = nrm(ks[11], (DEPTH, G, H), 1.0)
    ssm_log_dt = jax.random.uniform(ks[12], (DEPTH, G), F32, math.log(DT_MIN), math.log(DT_MAX))
    ssm_w_glu = nrm(ks[13], (DEPTH, SSM_WIDTH, SSM_WIDTH), SSM_WIDTH ** -0.5)
    ssm_b_glu = nrm(ks[14], (DEPTH, SSM_WIDTH), 0.01)
    attn_sinks = nrm(ks[15], (DEPTH, ATTN_Q_HEADS), 1.0)
    w_out = nrm(ks[16], (DEPTH, MIX_WIDTH, D_MODEL), MIX_WIDTH ** -0.5)
    norm_cross = 1.0 + nrm(ks[17], (DEPTH, D_MODEL), 0.02)
    norm_mem = 1.0 + nrm(ks[18], (DEPTH, D_MODEL), 0.02)
    w_cq = nrm(ks[19], (DEPTH, D_MODEL, MEM_WIDTH), D_MODEL ** -0.5)
    w_ckv = nrm(ks[20], (DEPTH, D_MODEL, 2 * MEM_WIDTH), D_MODEL ** -0.5)
    w_co = nrm(ks[21], (DEPTH, MEM_WIDTH, D_MODEL), MEM_WIDTH ** -0.5)
    norm_ffn = 1.0 + nrm(ks[22], (DEPTH, D_MODEL), 0.02)
    peer_w_q = nrm(ks[23], (DEPTH, D_MODEL, PEER_HEADS * PEER_KEY_DIM), D_MODEL ** -0.5)
    peer_sub_keys = nrm(ks[24], (DEPTH, PEER_HEADS, 2, PEER_N_KEYS, PEER_HALF), PEER_HALF ** -0.5)
    peer_u = nrm(ks[25], (DEPTH, PEER_N_EXPERTS, D_MODEL), D_MODEL ** -0.5)
    peer_v = nrm(ks[26], (DEPTH, PEER_N_EXPERTS, D_MODEL), 0.1)
    norm_final = 1.0 + nrm(ks[27], (D_MODEL,), 0.02)
    return {'x': x, 'mem': mem, 'rel_bias': rel_bias, 'norm_mix': norm_mix, 'w_in': w_in,
            'ssm_lambda_re': ssm_lambda_re, 'ssm_lambda_im': ssm_lambda_im,
            'ssm_b_re': ssm_b_re, 'ssm_b_im': ssm_b_im, 'ssm_c_re': ssm_c_re, 'ssm_c_im': ssm_c_im,
            'ssm_d': ssm_d, 'ssm_log_dt': ssm_log_dt, 'ssm_w_glu': ssm_w_glu, 'ssm_b_glu': ssm_b_glu,
            'attn_sinks': attn_sinks, 'w_out': w_out, 'norm_cross': norm_cross, 'norm_mem': norm_mem,
            'w_cq': w_cq, 'w_ckv': w_ckv, 'w_co': w_co, 'norm_ffn': norm_ffn,
            'peer_w_q': peer_w_q, 'peer_sub_keys': peer_sub_keys, 'peer_u': peer_u, 'peer_v': peer_v,
            'norm_final': norm_final}


def reference(x, mem, rel_bias, norm_mix, w_in, ssm_lambda_re, ssm_lambda_im, ssm_b_re, ssm_b_im,
              ssm_c_re, ssm_c_im, ssm_d, ssm_log_dt, ssm_w_glu, ssm_b_glu, attn_sinks, w_out,
              norm_cross, norm_mem, w_cq, w_ckv, w_co, norm_ffn, peer_w_q, peer_sub_keys,
              peer_u, peer_v, norm_final):
    bsz, L, _ = x.shape
    h = x
    for l in range(DEPTH):
        proj = rmsnorm(h, norm_mix[l]) @ w_in[l]
        u_ssm = proj[..., :SSM_WIDTH]
        o = SSM_WIDTH
        q = proj[..., o:o + ATTN_Q_WIDTH].reshape(bsz, L, ATTN_Q_HEADS, ATTN_HEAD_DIM)
        o += ATTN_Q_WIDTH
        k = proj[..., o:o + ATTN_KV_WIDTH].reshape(bsz, L, ATTN_KV_HEADS, ATTN_HEAD_DIM)
        o += ATTN_KV_WIDTH
        v = proj[..., o:o + ATTN_KV_WIDTH].reshape(bsz, L, ATTN_KV_HEADS, ATTN_HEAD_DIM)
        y_ssm = s5_mixer(u_ssm, ssm_lambda_re[l], ssm_lambda_im[l], ssm_b_re[l], ssm_b_im[l],
                         ssm_c_re[l], ssm_c_im[l], ssm_d[l], ssm_log_dt[l], ssm_w_glu[l], ssm_b_glu[l])
        y_attn = sliding_window_gqa_sinks(q, k, v, attn_sinks[l], rel_bias)
        y_mix = jnp.concatenate([y_ssm.astype(h.dtype), y_attn.astype(h.dtype)], axis=-1)
        h = h + y_mix @ w_out[l]
        h = h + memory_cross_attention(rmsnorm(h, norm_cross[l]), rmsnorm(mem, norm_mem[l]),
                                       w_cq[l], w_ckv[l], w_co[l])
        h = h + peer_ffn(rmsnorm(h, norm_ffn[l]), peer_w_q[l], peer_sub_keys[l], peer_u[l], peer_v[l])
    return rmsnorm(h, norm_final)
```

```python
import math
from contextlib import ExitStack
import numpy as np
import concourse.bass as bass
import concourse.mybir as mybir
from concourse.bass_utils import run_bass_kernel_spmd

F32 = mybir.dt.float32
BF16 = mybir.dt.bfloat16
U32 = mybir.dt.uint32
I32 = mybir.dt.int32
AF = mybir.ActivationFunctionType
ALU = mybir.AluOpType
AX = mybir.AxisListType

D = 2048
SEQ = 4096
NCORES = 8
EPS = 1e-6
TC = 32
SAME_ENGINE_SYNC = True


class Buf:
    __slots__ = ("name", "w", "r", "dsem", "dcnt")

    def __init__(self, name):
        self.name = name
        self.w = None
        self.r = []
        self.dsem = None
        self.dcnt = 0


class Eng:
    def __init__(self, name, h, sem):
        self.name = name
        self.h = h
        self.sem = sem
        self.cnt = 0
        self.waited = {}


class KB:
    def __init__(self, nc, es):
        self.nc = nc
        self.es = es
        self.eng = {}
        for name, h in (("pe", nc.tensor), ("dve", nc.vector), ("act", nc.scalar),
                        ("pool", nc.gpsimd), ("sp", nc.sync)):
            self.eng[name] = Eng(name, h, es.enter_context(nc.semaphore("sem_" + name)))
        self.nsem = 5

    def _wait(self, e, tok, raw=True):
        if tok is None:
            return
        sem, val = tok
        if sem is e.sem and (e.name == "pe" or not SAME_ENGINE_SYNC or not raw):
            return
        k = id(sem)
        if e.waited.get(k, 0) >= val:
            return
        e.h.wait_ge(sem, val)
        e.waited[k] = val

    def _deps(self, e, reads, writes):
        for b in reads:
            self._wait(e, b.w)
        for b in writes:
            self._wait(e, b.w)
            for t in b.r:
                self._wait(e, t)

    def _commit(self, tok, reads, writes):
        for b in reads:
            b.r = [t for t in b.r if t[0] is not tok[0]] + [tok]
        for b in writes:
            b.w = tok
            b.r = []

    def op(self, ename, reads, writes, fn):
        e = self.eng[ename]
        self._deps(e, reads, writes)
        ins = fn()
        if isinstance(ins, (list, tuple)):
            ins = ins[-1]
        e.cnt += 1
        ins.then_inc(e.sem, 1)
        self._commit((e.sem, e.cnt), reads, writes)

    def dma(self, qname, sbuf_buf, reads, writes, fn):
        e = self.eng[qname]
        self._deps(e, reads, writes)
        if sbuf_buf.dsem is None:
            sbuf_buf.dsem = self.es.enter_context(self.nc.semaphore("d_" + sbuf_buf.name))
            self.nsem += 1
        ins = fn()
        sbuf_buf.dcnt += 16
        ins.then_inc(sbuf_buf.dsem, 16)
        self._commit((sbuf_buf.dsem, sbuf_buf.dcnt), reads, writes)

    def wait_all(self, ename, bufs):
        e = self.eng[ename]
        for b in bufs:
            self._wait(e, b.w)
            for t in b.r:
                self._wait(e, t)


def V(t, off, dims, p0=0, npart=128):
    a = t[:]
    ps = a.ap[0][0]
    return bass.AP(tensor=a.tensor, offset=a.offset + p0 * ps + off,
                   ap=[[ps, npart]] + [list(d) for d in dims])


def DV(d, off, dims):
    return bass.AP(tensor=d.tensor, offset=d.offset + off, ap=[list(x) for x in dims])


def build(NT=32, debug=False, stop=99, phases="ABCD"):
    L = NT * 128
    nc = bass.Bass("TRN2", target_bir_lowering=False)
    es = ExitStack()
    kb = KB(nc, es)

    def din(name, shape, dt=F32):
        return nc.dram_tensor(name, list(shape), dt, kind="ExternalInput").ap()

    def dscr(name, shape, dt):
        return nc.dram_tensor(name, list(shape), dt, kind="Internal").ap()

    x = din("x", [L, D])
    mem = din("mem", [256, D])
    rel_bias = din("rel_bias", [32, 16])
    norm_mix = din("norm_mix", [1, D])
    w_in = din("w_in", [D, 2816])
    lam_re = din("lam_re", [64, 64])
    lam_im = din("lam_im", [64, 64])
    b_re = din("b_re", [64, 64, 16])
    b_im = din("b_im", [64, 64, 16])
    c_re = din("c_re", [64, 16, 64])
    c_im = din("c_im", [64, 16, 64])
    ssm_d = din("ssm_d", [1, 1024])
    log_dt = din("log_dt", [1, 64])
    w_glu = din("w_glu", [1024, 1024])
    b_glu = din("b_glu", [1, 1024])
    sinks = din("sinks", [1, 16])
    w_out = din("w_out", [D, D])
    norm_cross = din("norm_cross", [1, D])
    norm_mem = din("norm_mem", [1, D])
    w_cq = din("w_cq", [D, 512])
    w_ckv = din("w_ckv", [D, 1024])
    w_co = din("w_co", [512, D])
    norm_ffn = din("norm_ffn", [1, D])
    w_pq = din("w_pq", [D, D])
    sub_keys = din("sub_keys", [16, 128, 128])
    peer_u = din("peer_u", [16384, D])
    peer_v = din("peer_v", [16384, D])
    norm_final = din("norm_final", [1, D])
    ebias = din("ebias", [128, 32 * 256])
    negmask = din("negmask", [128, 256])
    ident_d = din("ident", [128, 128])
    iota16 = din("iota16", [1, 16])
    out = nc.dram_tensor("out", [L, D], F32, kind="ExternalOutput").ap()

    u_scr = dscr("u_scr", [1024, L], BF16)
    ya_scr = dscr("ya_scr", [L, 1024], BF16)
    ys_scr = dscr("ys_scr", [1024, L], BF16)
    h2_scr = dscr("h2_scr", [L, D], F32)
    u_bf = dscr("u_bf", [16384, D], BF16)
    v_bf = dscr("v_bf", [16384, D], BF16)
    cvb = [Buf("cv0"), Buf("cv1")]
    B_u = Buf("u_scr"); B_ya = Buf("ya_scr"); B_ys = Buf("ys_scr"); B_h2 = Buf("h2_scr")
    B_out = Buf("out")
    dbg = {}
    if debug:
        dbg["h2"] = h2_scr

    def sb(name, shape, dt, stack=None):
        t = (stack or es).enter_context(nc.sbuf_tensor(name, list(shape), dt))
        return t, Buf(name)

    def ps(name, shape, dt, stack=None):
        t = (stack or es).enter_context(nc.psum_tensor(name, list(shape), dt))
        return t, Buf(name)

    def bcast_load(q, t, tb, src, n):
        kb.dma(q, tb, [], [tb], lambda: kb.eng[q].h.dma_start(
            out=t[:], in_=DV(src, 0, [[0, 128], [1, n]])))

    ident_f, B_idf = sb("ident_f", [128, 128], F32)
    ident_b, B_idb = sb("ident_b", [128, 128], BF16)
    kb.dma("sp", B_idf, [], [B_idf], lambda: nc.sync.dma_start(out=ident_f[:], in_=ident_d[:, :]))
    kb.op("dve", [B_idf], [B_idb], lambda: nc.vector.tensor_copy(out=ident_b[:], in_=ident_f[:]))

    def rmsnorm_tile(xt, B_x, g_bc, B_g, outs, ss, B_ss, junk, B_junk):
        kb.op("act", [B_x], [B_junk, B_ss], lambda: nc.scalar.activation(
            out=junk[:], in_=xt, func=AF.Square, accum_out=ss[:, 0:1]))
        kb.op("act", [B_ss], [B_ss], lambda: nc.scalar.activation(
            out=ss[:, 1:2], in_=ss[:, 0:1], func=AF.Sqrt, scale=1.0 / D, bias=EPS))
        kb.op("dve", [B_ss], [B_ss], lambda: nc.vector.reciprocal(out=ss[:, 2:3], in_=ss[:, 1:2]))
        for o_ap, B_o in outs:
            kb.op("dve", [B_x, B_ss, B_g], [B_o], lambda o_ap=o_ap: nc.vector.scalar_tensor_tensor(
                out=o_ap, in0=xt, scalar=ss[:, 2:3], in1=g_bc[:], op0=ALU.mult, op1=ALU.mult))

    def transposes(src_t, B_src, n, pst, B_pst, dst, B_dst, evac="act"):
        kb.op("pe", [B_src, B_idb], [B_pst], lambda: [
            nc.tensor.transpose(out=pst[:, k, :], in_=src_t[:, k * 128:(k + 1) * 128], identity=ident_b[:])
            for k in range(n)])
        dst_ap = dst if isinstance(dst, bass.AP) else dst[:, 0:n, :]
        if evac == "act":
            kb.op("act", [B_pst], [B_dst], lambda: nc.scalar.copy(out=dst_ap, in_=pst[:, 0:n, :]))
        else:
            kb.op("dve", [B_pst], [B_dst], lambda: nc.vector.tensor_copy(out=dst_ap, in_=pst[:, 0:n, :]))

    def load_w_bf16(t, B_t, src, nk, ncols):
        for c0 in range(0, ncols, 2048):
            c1 = min(ncols, c0 + 2048)
            for k0 in range(0, nk, 4):
                k1 = min(nk, k0 + 4)
                kb.dma("pool", B_t, [], [B_t], lambda c0=c0, c1=c1, k0=k0, k1=k1: nc.gpsimd.dma_start(
                    out=t[:, k0:k1, c0:c1],
                    in_=DV(src, k0 * 128 * ncols + c0, [[ncols, 128], [128 * ncols, k1 - k0], [1, c1 - c0]])))

    def convert_tables():
        k = 0
        for src, dst in ((peer_u, u_bf), (peer_v, v_bf)):
            for r0 in range(0, 16384, 4096):
                c = cvb[k % 2]
                k += 1
                kb.dma("pool", c, [], [c], lambda src=src, dst=dst, r0=r0: nc.gpsimd.dma_start(
                    out=dst[r0:r0 + 4096, :], in_=src[r0:r0 + 4096, :]))

    def phase_a(pa):
        w_in_sb, B_win = sb("w_in_sb", [128, 16, 2816], BF16, pa)
        load_w_bf16(w_in_sb, B_win, w_in, 16, 2816)
        g_mix, B_gmix = sb("g_mix", [128, D], F32, pa)
        bcast_load("sp", g_mix, B_gmix, norm_mix, D)
        biasT, B_bias = sb("biasT", [128, 16, 256], F32, pa)
        esink, B_esink = sb("esink", [128, 16], F32, pa)
        bcast_load("sp", esink, B_esink, sinks, 16)
        kb.op("act", [B_esink], [B_esink], lambda: nc.scalar.activation(out=esink[:], in_=esink[:], func=AF.Exp))
        with ExitStack() as s0:
            E_sb, B_E = sb("E_sb", [128, 32 * 256], F32, s0)
            rb, B_rb = sb("rb", [128, 512], F32, s0)
            kb.dma("sp", B_E, [], [B_E], lambda: nc.sync.dma_start(out=E_sb[:], in_=ebias[:, :]))
            kb.dma("sp", B_rb, [], [B_rb], lambda: nc.sync.dma_start(
                out=rb[:], in_=DV(rel_bias, 0, [[0, 128], [1, 512]])))
            def bslice(hd):
                return biasT[:, hd, :].rearrange("p (a b) -> p a b", a=2)
            for hd in range(16):
                kb.dma("sp", B_bias, [], [B_bias], lambda hd=hd: nc.sync.dma_start(
                    out=bslice(hd), in_=negmask[:, :].rearrange("p (a b) -> p a b", a=2)))
            for hd in range(16):
                for b in range(32):
                    kb.op("dve", [B_E, B_rb, B_bias], [B_bias], lambda hd=hd, b=b: nc.vector.scalar_tensor_tensor(
                        out=bslice(hd), in0=E_sb[:, b * 256:(b + 1) * 256].rearrange("p (a b) -> p a b", a=2),
                        scalar=rb[:, b * 16 + hd:b * 16 + hd + 1], in1=bslice(hd),
                        op0=ALU.mult, op1=ALU.add))
            kb.wait_all("sp", [B_E, B_rb])

        xb = [sb(f"xa{i}", [128, D], F32, pa) for i in range(2)]
        ss, B_ss = sb("ssa", [128, 4], F32, pa)
        junk, B_junk = sb("junka", [128, D], BF16, pa)
        xn, B_xn = sb("xna", [128, D], BF16, pa)
        xnT, B_xnT = sb("xnTa", [128, 16, 128], BF16, pa)
        uT, B_uT = sb("uTa", [128, 8, 128], BF16, pa)
        QTs = [sb(f"QTa{i}", [128, 8, 2, 128], BF16, pa) for i in range(2)]
        kT2 = [sb(f"kT2a{i}", [128, 4, 128], BF16, pa) for i in range(3)]
        Vg = [sb(f"Vga{i}", [128, 4, 65], BF16, pa) for i in range(3)]
        sc = [sb(f"sca{i}", [128, 512], F32, pa) for i in range(2)]
        pT = [sb(f"pTa{i}", [128, 512], BF16, pa) for i in range(2)]
        dens = [sb(f"dena{i}", [128, 4], F32, pa) for i in range(2)]
        B_psOp = [Buf("psOp0"), Buf("psOp1")]
        ya = [sb(f"yaa{i}", [128, 1024], BF16, pa) for i in range(2)]
        pst, B_pst = ps("pstA", [128, 16, 128], BF16, pa)
        psA, B_psA = ps("psA", [128, 1024], F32, pa)
        psB, B_psB = psA, B_psA
        psS = [ps(f"psS{i}", [128, 2, 256], F32, pa) for i in range(2)]
        B_pSp = [Buf("pSp0"), Buf("pSp1")]
        psO, B_psO = ps("psO", [128, 2, 2, 65], F32, pa)
        for i in range(3):
            kb.op("dve", [], [Vg[i][1]], lambda i=i: nc.vector.memset(Vg[i][0][:, :, 64:65], 1.0))
        for i in range(2):
            kb.op("dve", [], [QTs[i][1]], lambda i=i: nc.vector.memset(QTs[i][0][:], 0.0))

        kb.dma("sp", xb[0][1], [], [xb[0][1]], lambda: nc.sync.dma_start(out=xb[0][0][:], in_=x[0:128, :]))

        def first_half(i):
            xt, B_x = xb[i % 2]
            QT, B_QT = QTs[i % 2]
            kt, B_kt = kT2[i % 3]
            vg, B_vg = Vg[i % 3]
            if i + 1 < NT:
                nt_, B_n = xb[(i + 1) % 2]
                kb.dma("sp", B_n, [], [B_n], lambda: nc.sync.dma_start(
                    out=nt_[:], in_=x[(i + 1) * 128:(i + 2) * 128, :]))
            rmsnorm_tile(xt[:], B_x, g_mix, B_gmix, [(xn[:], B_xn)], ss, B_ss, junk, B_junk)
            yield
            transposes(xn, B_xn, 16, pst, B_pst, xnT, B_xnT)
            yield
            for half in range(2):
                kb.op("pe", [B_win, B_xnT], [B_psA], lambda half=half: [
                    nc.tensor.matmul(psA[:, c * 128:(c + 1) * 128], lhsT=w_in_sb[:, kc, c * 128:(c + 1) * 128],
                                     rhs=xnT[:, kc, :], start=(kc == 0), stop=(kc == 15))
                    for c in range(4 * half, 4 * half + 4) for kc in range(16)])
                yield
            kb.op("dve", [B_psA], [B_uT], lambda: nc.vector.tensor_copy(
                out=uT[:].rearrange("p a b -> p (a b)"), in_=psA[:]))
            kb.dma("sp", B_uT, [B_uT], [B_u], lambda: nc.sync.dma_start(
                out=DV(u_scr, i * 128, [[L, 128], [128 * L, 8], [1, 128]]), in_=uT[:]))
            yield
            for half in range(2):
                kb.op("pe", [B_win, B_xnT], [B_psA], lambda half=half: [
                    nc.tensor.matmul(psA[:, c * 128:(c + 1) * 128],
                                     lhsT=w_in_sb[:, kc, 1024 + c * 128:1024 + (c + 1) * 128],
                                     rhs=xnT[:, kc, :], start=(kc == 0), stop=(kc == 15))
                    for c in range(4 * half, 4 * half + 4) for kc in range(16)])
                yield
            for hf in range(2):
                kb.op("act", [B_psA], [B_QT], lambda hf=hf: nc.scalar.copy(
                    out=QT[hf * 64:(hf + 1) * 64, :, hf, :],
                    in_=psA[hf * 64:(hf + 1) * 64, :].rearrange("p (a b) -> p a b", a=8)))
            yield
            kb.op("pe", [B_win, B_xnT], [B_psA], lambda: [
                nc.tensor.matmul(psA[:, c * 128:(c + 1) * 128], lhsT=w_in_sb[:, kc, 2048 + c * 128:2048 + (c + 1) * 128],
                                 rhs=xnT[:, kc, :], start=(kc == 0), stop=(kc == 15))
                for c in range(4) for kc in range(16)] + [
                nc.tensor.matmul(psA[:, 512:768], lhsT=xnT[:, kc, :], rhs=w_in_sb[:, kc, 2560:2816],
                                 start=(kc == 0), stop=(kc == 15)) for kc in range(16)])
            kb.op("act", [B_psA], [B_kt], lambda: nc.scalar.copy(
                out=kt[:].rearrange("p a b -> p (a b)"), in_=psA[:, 0:512]))
            kb.op("dve", [B_psA], [B_vg], lambda: nc.vector.tensor_copy(
                out=vg[:, :, 0:64], in_=psA[:, 512:768].rearrange("p (g d) -> p g d", g=4)))
            yield

        def swa(i, gen):
            QT, B_QT = QTs[i % 2]
            yat, B_yat = ya[i % 2]
            kbs = [1] if i == 0 else [0, 1]

            def kvb(kb_):
                return (i + kb_ - 1) % 3

            def emit_scores(gp):
                g = gp // 2
                par = gp % 2
                rd = [B_QT] + [kT2[kvb(kb_)][1] for kb_ in kbs]
                kb.op("pe", rd, [B_pSp[par]], lambda: [
                    nc.tensor.matmul(psS[hf][0][:, par, kb_ * 128:(kb_ + 1) * 128],
                                     lhsT=kT2[kvb(kb_)][0][:, g, :],
                                     rhs=QT[:, gp, hf, :], start=True, stop=True)
                    for kb_ in kbs for hf in range(2)])

            emit_scores(0)
            for gp in range(8):
                g = gp // 2
                par = gp % 2
                if gp + 1 < 8:
                    emit_scores(gp + 1)
                B_pS = B_pSp[par]
                sct, B_sct = sc[par]
                pt, B_pt = pT[par]
                B_pO = B_psOp[par]
                dn, B_dn = dens[par]
                c0 = 128 * kbs[0]
                for hf in range(2):
                    kb.op("dve", [B_pS, B_bias], [B_sct], lambda hf=hf: nc.vector.scalar_tensor_tensor(
                        out=sct[:, hf * 256 + c0:(hf + 1) * 256], in0=psS[hf][0][:, par, c0:256], scalar=0.125,
                        in1=biasT[:, 2 * gp + hf, c0:256], op0=ALU.mult, op1=ALU.add))
                if c0:
                    kb.op("act", [B_sct], [B_pt], lambda: [nc.scalar.activation(
                        out=pt[:, hf * 256 + 128:(hf + 1) * 256], in_=sct[:, hf * 256 + 128:(hf + 1) * 256], func=AF.Exp)
                        for hf in range(2)])
                else:
                    kb.op("act", [B_sct], [B_pt], lambda: nc.scalar.activation(
                        out=pt[:], in_=sct[:], func=AF.Exp))
                rd = [B_pt] + [Vg[kvb(kb_)][1] for kb_ in kbs]
                kb.op("pe", rd, [B_pO], lambda: [
                    nc.tensor.matmul(psO[:, gp % 2, hf, :],
                                     lhsT=pt[:, (hf * 2 + kb_) * 128:(hf * 2 + kb_ + 1) * 128],
                                     rhs=Vg[kvb(kb_)][0][:, g, :],
                                     start=(kb_ == kbs[0]), stop=(kb_ == kbs[-1]))
                    for hf in range(2) for kb_ in kbs])
                kb.op("dve", [B_pO, B_esink], [B_dn], lambda: nc.vector.tensor_tensor(
                    out=dn[:, 0:2], in0=psO[:, gp % 2, :, 64], in1=esink[:, 2 * gp:2 * gp + 2], op=ALU.add))
                kb.op("dve", [B_dn], [B_dn], lambda: nc.vector.reciprocal(out=dn[:, 2:4], in_=dn[:, 0:2]))
                for hf in range(2):
                    kb.op("dve", [B_pO, B_dn], [B_yat], lambda hf=hf: nc.vector.tensor_scalar(
                        out=yat[:, (2 * gp + hf) * 64:(2 * gp + hf + 1) * 64], in0=psO[:, gp % 2, hf, 0:64],
                        scalar1=dn[:, 2 + hf:3 + hf], scalar2=None, op0=ALU.mult))
                if gen is not None:
                    next(gen, None)
                    next(gen, None)
            if gen is not None:
                for _ in gen:
                    pass
            kb.dma("sp", B_yat, [B_yat], [B_ya], lambda: nc.sync.dma_start(
                out=ya_scr[i * 128:(i + 1) * 128, :], in_=yat[:]))

        for _ in first_half(0):
            pass
        for i in range(NT):
            swa(i, first_half(i + 1) if i + 1 < NT else None)
        allb = [B_win, B_gmix, B_bias, B_esink, B_ss, B_junk, B_xn, B_xnT, B_uT, B_pst, B_psA,
                B_psO] + B_pSp + B_psOp + [b for _, b in xb + kT2 + Vg + sc + pT + ya + dens + QTs]
        for en in ("pe", "dve", "act", "pool", "sp"):
            kb.wait_all(en, allb)


    TWO_PI = 2.0 * math.pi
    def phase_b(pb):
        BbT, B_BbT = sb("BbT", [128, 32, 2, 128], BF16, pb)
        CL, B_CL = sb("CL", [128, 32, 2, 128], BF16, pb)
        Ct, B_tab = sb("Ct", [128, 32, TC], F32, pb)
        St, _ = sb("St", [128, 32, TC], F32, pb)
        rtab, _ = sb("rtab", [128, 32, TC], F32, pb)
        lbr, _ = sb("lbr", [128, 32], F32, pb)
        lbi, _ = sb("lbi", [128, 32], F32, pb)
        Kr, _ = sb("s5Kr", [128, 32], F32, pb)
        Ki, _ = sb("s5Ki", [128, 32], F32, pb)
        Ctb, _ = sb("Ctb", [128, 32 * TC], BF16, pb)
        Stb, _ = sb("Stb", [128, 32 * TC], BF16, pb)
        Dg, B_Dg = sb("Dg", [128, 8, 128], BF16, pb)
        bg, B_bg = sb("bg", [128, 8], F32, pb)
        wg, B_wg = sb("wg", [128, 8, 1024], BF16, pb)
        load_w_bf16(wg, B_wg, w_glu, 8, 1024)
        if "D" in phases:
            convert_tables()
        kb.dma("sp", B_bg, [], [B_bg], lambda: nc.sync.dma_start(
            out=bg[:], in_=DV(b_glu, 0, [[1, 128], [128, 8]]), allow_slow_non_contiguous=True))
        with ExitStack() as s1:
            B_su = Buf("s5setup")
            cnt = [0]

            def st(shape, dt=F32):
                cnt[0] += 1
                return s1.enter_context(nc.sbuf_tensor(f"s5t{cnt[0]}", list(shape), dt))

            def dv(fn):
                kb.op("dve", [B_su], [B_su], fn)

            def ac(fn):
                kb.op("act", [B_su], [B_su], fn)

            def tt(o, a, b, op):
                dv(lambda: nc.vector.tensor_tensor(out=o, in0=a, in1=b, op=op))

            def ts(o, a, sc_, op):
                dv(lambda: nc.vector.tensor_single_scalar(out=o, in_=a, scalar=sc_, op=op))

            lr = st([128, 32]); li = st([128, 32]); ldt = st([128, 32])
            kb.dma("sp", B_su, [], [B_su], lambda: nc.sync.dma_start(
                out=lr[:], in_=DV(lam_re, 0, [[1, 128], [128, 32]]), allow_slow_non_contiguous=True))
            kb.dma("sp", B_su, [], [B_su], lambda: nc.sync.dma_start(
                out=li[:], in_=DV(lam_im, 0, [[1, 128], [128, 32]]), allow_slow_non_contiguous=True))
            for a in range(2):
                kb.dma("sp", B_su, [], [B_su], lambda a=a: nc.sync.dma_start(
                    out=ldt[a * 64:(a + 1) * 64, :], in_=DV(log_dt, a, [[0, 64], [2, 32]]),
                    allow_slow_non_contiguous=True))
            bre = st([128, 32, 16]); bim = st([128, 32, 16])
            kb.dma("sp", B_su, [], [B_su], lambda: nc.sync.dma_start(
                out=bre[:], in_=DV(b_re, 0, [[16, 128], [2048, 32], [1, 16]])))
            kb.dma("sp", B_su, [], [B_su], lambda: nc.sync.dma_start(
                out=bim[:], in_=DV(b_im, 0, [[16, 128], [2048, 32], [1, 16]])))
            Cd = [st([128, 8, 128]) for _ in range(2)]
            for ri, csrc in enumerate((c_re, c_im)):
                for dup in range(2):
                    kb.dma("sp", B_su, [], [B_su], lambda ri=ri, csrc=csrc, dup=dup: nc.sync.dma_start(
                        out=Cd[ri][:, :, dup * 64:(dup + 1) * 64], in_=DV(csrc, 0, [[64, 128], [8192, 8], [1, 64]])))
            Dt = st([128, 8])
            kb.dma("sp", B_su, [], [B_su], lambda: nc.sync.dma_start(
                out=Dt[:], in_=DV(ssm_d, 0, [[1, 128], [128, 8]]), allow_slow_non_contiguous=True))

            dtt = st([128, 32]); rho = st([128, 32]); th = st([128, 32]); mag = st([128, 32])
            ac(lambda: nc.scalar.activation(out=dtt[:], in_=ldt[:], func=AF.Exp))
            tt(rho[:], lr[:], dtt[:], ALU.mult)
            tt(th[:], li[:], dtt[:], ALU.mult)
            ac(lambda: nc.scalar.activation(out=mag[:], in_=rho[:], func=AF.Exp))

            def sin_of(x_ap, shift, o_ap):
                t = st([128, 32]); ki = st([128, 32], I32); kf = st([128, 32]); r = st([128, 32]); m = st([128, 32])
                xs = st([128, 32])
                ts(xs[:], x_ap, shift, ALU.add)
                ts(t[:], xs[:], 1.0 / TWO_PI, ALU.mult)
                dv(lambda: nc.vector.tensor_copy(out=ki[:], in_=t[:]))
                dv(lambda: nc.vector.tensor_copy(out=kf[:], in_=ki[:]))
                dv(lambda: nc.vector.scalar_tensor_tensor(out=r[:], in0=kf[:], scalar=-TWO_PI, in1=xs[:],
                                                          op0=ALU.mult, op1=ALU.add))
                ts(m[:], r[:], math.pi, ALU.is_gt)
                dv(lambda: nc.vector.scalar_tensor_tensor(out=r[:], in0=m[:], scalar=-TWO_PI, in1=r[:],
                                                          op0=ALU.mult, op1=ALU.add))
                ts(m[:], r[:], -math.pi, ALU.is_lt)
                dv(lambda: nc.vector.scalar_tensor_tensor(out=r[:], in0=m[:], scalar=TWO_PI, in1=r[:],
                                                          op0=ALU.mult, op1=ALU.add))
                ts(r[:], r[:], math.pi, ALU.min)
                ts(r[:], r[:], -math.pi, ALU.max)
                ac(lambda: nc.scalar.activation(out=o_ap, in_=r[:], func=AF.Sin))

            sn = st([128, 32]); cs = st([128, 32])
            sin_of(th[:], 0.0, sn[:])
            sin_of(th[:], math.pi / 2, cs[:])
            tt(lbr[:], mag[:], cs[:], ALU.mult)
            tt(lbi[:], mag[:], sn[:], ALU.mult)
            nr = st([128, 32]); dd = st([128, 32]); t1_ = st([128, 32]); t2_ = st([128, 32])
            cr = st([128, 32]); ci = st([128, 32])
            ts(nr[:], lbr[:], -1.0, ALU.add)
            tt(t1_[:], lr[:], lr[:], ALU.mult)
            tt(t2_[:], li[:], li[:], ALU.mult)
            tt(dd[:], t1_[:], t2_[:], ALU.add)
            dv(lambda: nc.vector.reciprocal(out=dd[:], in_=dd[:]))
            tt(t1_[:], nr[:], lr[:], ALU.mult)
            tt(t2_[:], lbi[:], li[:], ALU.mult)
            tt(cr[:], t1_[:], t2_[:], ALU.add)
            tt(cr[:], cr[:], dd[:], ALU.mult)
            tt(t1_[:], lbi[:], lr[:], ALU.mult)
            tt(t2_[:], nr[:], li[:], ALU.mult)
            tt(ci[:], t1_[:], t2_[:], ALU.subtract)
            tt(ci[:], ci[:], dd[:], ALU.mult)
            bbr = st([128, 32, 16]); bbi = st([128, 32, 16]); tb1 = st([128, 32, 16]); tb2 = st([128, 32, 16])
            crb = V(cr, 0, [[1, 32], [0, 16]]); cib = V(ci, 0, [[1, 32], [0, 16]])
            tt(tb1[:], bre[:], crb, ALU.mult)
            tt(tb2[:], bim[:], cib, ALU.mult)
            tt(bbr[:], tb1[:], tb2[:], ALU.subtract)
            tt(tb1[:], bim[:], crb, ALU.mult)
            tt(tb2[:], bre[:], cib, ALU.mult)
            tt(bbi[:], tb1[:], tb2[:], ALU.add)
            dv(lambda: nc.vector.memset(Ct[:, :, 0:1], 1.0))
            dv(lambda: nc.vector.memset(St[:, :, 0:1], 0.0))
            cn = st([128, 32]); sn2 = st([128, 32]); tn = st([128, 32])
            dv(lambda: nc.vector.tensor_copy(out=cn[:], in_=cs[:]))
            dv(lambda: nc.vector.tensor_copy(out=sn2[:], in_=sn[:]))
            ta = st([128, 32, TC]); tbb = st([128, 32, TC])
            n = 1
            while n < TC:
                cnb = V(cn, 0, [[1, 32], [0, n]]); snb = V(sn2, 0, [[1, 32], [0, n]])
                tt(ta[:, :, 0:n], Ct[:, :, 0:n], cnb, ALU.mult)
                tt(tbb[:, :, 0:n], St[:, :, 0:n], snb, ALU.mult)
                tt(Ct[:, :, n:2 * n], ta[:, :, 0:n], tbb[:, :, 0:n], ALU.subtract)
                tt(ta[:, :, 0:n], St[:, :, 0:n], cnb, ALU.mult)
                tt(tbb[:, :, 0:n], Ct[:, :, 0:n], snb, ALU.mult)
                tt(St[:, :, n:2 * n], ta[:, :, 0:n], tbb[:, :, 0:n], ALU.add)
                tt(tn[:], cn[:], sn2[:], ALU.mult)
                tt(t1_[:], cn[:], cn[:], ALU.mult)
                tt(t2_[:], sn2[:], sn2[:], ALU.mult)
                tt(cn[:], t1_[:], t2_[:], ALU.subtract)
                ts(sn2[:], tn[:], 2.0, ALU.mult)
                n *= 2
            tt(Kr[:], mag[:], cn[:], ALU.mult)
            tt(Ki[:], mag[:], sn2[:], ALU.mult)
            dv(lambda: nc.vector.tensor_copy(out=Ctb[:], in_=Ct[:].rearrange("p a b -> p (a b)")))
            dv(lambda: nc.vector.tensor_copy(out=Stb[:], in_=St[:].rearrange("p a b -> p (a b)")))
            dv(lambda: nc.vector.tensor_copy(out=rtab[:], in_=V(mag, 0, [[1, 32], [0, TC]])))
            dv(lambda: nc.vector.memset(rtab[:, :, 0:1], 0.0))
            for t in range(8):
                kb.op("dve", [B_su, B_idf], [B_su, B_Dg], lambda t=t: nc.vector.tensor_scalar(
                    out=Dg[:, t, :], in0=ident_f[:], scalar1=Dt[:, t:t + 1], scalar2=None, op0=ALU.mult))
            Z = st([128, 32, 128])
            psT1, B_psT1 = ps("psT1", [128, 4, 128], F32, s1)
            for ri, bb in enumerate((bbr, bbi)):
                dv(lambda: nc.vector.memset(Z[:], 0.0))
                for a in range(2):
                    dv(lambda a=a, bb=bb: nc.vector.tensor_copy(
                        out=V(Z, 16 * a, [[512, 8], [160, 4], [1, 16]], p0=64 * a, npart=64),
                        in_=V(bb, 0, [[64, 8], [16, 4], [1, 16]], p0=64 * a, npart=64)))
                for j4 in range(8):
                    kb.op("pe", [B_su, B_idf], [B_psT1], lambda j4=j4: [
                        nc.tensor.transpose(out=psT1[:, jj, :], in_=Z[:, 4 * j4 + jj, :], identity=ident_f[:])
                        for jj in range(4)])
                    kb.op("act", [B_psT1], [B_BbT], lambda j4=j4, ri=ri: nc.scalar.copy(
                        out=BbT[:, 4 * j4:4 * j4 + 4, ri, :], in_=psT1[:]))
            kb.op("dve", [], [B_CL], lambda: nc.vector.memset(CL[:], 0.0))
            for ri in range(2):
                for t in range(8):
                    kb.op("pe", [B_su, B_idf], [B_psT1], lambda t=t, ri=ri: nc.tensor.transpose(
                        out=psT1[:, 0, :], in_=Cd[ri][:, t, :], identity=ident_f[:]))
                    for a in range(2):
                        kb.op("dve", [B_psT1], [B_CL], lambda t=t, ri=ri, a=a: nc.vector.tensor_scalar(
                            out=V(CL, (4 * t) * 256 + ri * 128 + 16 * a, [[288, 4], [1, 16]], p0=64 * a, npart=64),
                            in0=V(psT1, 16 * a, [[32, 4], [1, 16]], p0=64 * a, npart=64),
                            scalar1=(1.0 if ri == 0 else -1.0), scalar2=None, op0=ALU.mult))
            kb.op("dve", [B_su], [B_tab], lambda: nc.vector.tensor_copy(out=lbr[:], in_=lbr[:]))
            for en in ("pe", "dve", "act", "sp"):
                kb.wait_all(en, [B_su, B_psT1])

        NBLK = L // 512
        ub = [sb(f"ub{i}", [128, 8, 512], BF16, pb) for i in range(2)]
        tA, B_tA = sb("s5tA", [128, 1024], F32, pb)
        tB, B_tB = sb("s5tB", [128, 1024], F32, pb)
        tC, B_tC = sb("s5tC", [128, 1024], BF16, pb)
        tD, B_tD = sb("s5tD", [128, 1024], BF16, pb)
        zrb, B_zrb = sb("s5zrb", [128, 1024], BF16, pb)
        zib, B_zib = sb("s5zib", [128, 1024], BF16, pb)
        zr, B_zr = sb("s5zr", [128, 32, TC], F32, pb)
        zi, B_zi = sb("s5zi", [128, 32, TC], F32, pb)
        zrs2 = [sb(f"s5zrs{i}", [128, 1024], F32, pb) for i in range(2)]
        zis2 = [sb(f"s5zis{i}", [128, 1024], F32, pb) for i in range(2)]
        XRb, B_XRb = sb("s5XRb", [128, 32, TC], BF16, pb)
        XIb, B_XIb = sb("s5XIb", [128, 32, TC], BF16, pb)
        m1, B_m = sb("s5m1", [128, 32], F32, pb)
        m2, _ = sb("s5m2", [128, 32], F32, pb)
        yg = [sb(f"s5yg{i}", [128, 8, 512], BF16, pb) for i in range(2)]
        sig, B_sig = sb("s5sig", [128, 512], F32, pb)
        ysb, B_ysb = sb("s5ysb", [128, 8, 512], BF16, pb)
        psBU = [ps(f"psBU{i}", [128, 1024], F32, pb) for i in range(2)]
        psY, B_psY = ps("psY", [128, 8, TC], F32, pb)
        psZ = [ps(f"psZ{i}", [128, 512], F32, pb) for i in range(2)]
        Ctf = Ct[:].rearrange("p a b -> p (a b)")
        Stf = St[:].rearrange("p a b -> p (a b)")
        rtf = rtab[:].rearrange("p a b -> p (a b)")

        def vtt(o, B_o, a, B_a, b, B_b, op, eng="dve"):
            h = nc.vector if eng == "dve" else nc.gpsimd
            kb.op(eng, [B_a, B_b], [B_o], lambda: h.tensor_tensor(out=o, in0=a, in1=b, op=op))

        def load_ub(blk):
            t_, B_ = ub[blk % 2]
            kb.dma("sp", B_, [B_u], [B_], lambda: nc.sync.dma_start(
                out=t_[:], in_=DV(u_scr, blk * 512, [[L, 128], [128 * L, 8], [1, 512]])))

        load_ub(0)
        gchunk = 0
        for blk in range(NBLK):
            if blk + 1 < NBLK:
                load_ub(blk + 1)
            ubt, B_ub = ub[blk % 2]
            ygt, B_yg = yg[blk % 2]
            for ci in range(512 // TC):
                s0 = ci * TC
                par = gchunk % 2
                zrs_p, B_zrsp = zrs2[1 - par]; zis_p, B_zisp = zis2[1 - par]
                zrs, B_zrs = zrs2[par]; zis, B_zis = zis2[par]
                for ri in range(2):
                    kb.op("pe", [B_BbT, B_ub], [psBU[ri][1]], lambda ri=ri, s0=s0, ubt=ubt: [
                        nc.tensor.matmul(psBU[ri][0][:, j * TC:(j + 1) * TC], lhsT=BbT[:, j, ri, :],
                                         rhs=ubt[:, j // 4, s0:s0 + TC], start=True, stop=True)
                        for j in range(32)])
                bur, B_bur = psBU[0]; bui, B_bui = psBU[1]
                zrf = zr[:].rearrange("p a b -> p (a b)"); zif = zi[:].rearrange("p a b -> p (a b)")
                vtt(tA[:], B_tA, bur[:], B_bur, Ctf, B_tab, ALU.mult)
                vtt(tB[:], B_tB, bui[:], B_bui, Stf, B_tab, ALU.mult)
                vtt(zrf, B_zr, tA[:], B_tA, tB[:], B_tB, ALU.add)
                vtt(tA[:], B_tA, bui[:], B_bui, Ctf, B_tab, ALU.mult)
                vtt(tB[:], B_tB, bur[:], B_bur, Stf, B_tab, ALU.mult)
                vtt(zif, B_zi, tA[:], B_tA, tB[:], B_tB, ALU.subtract)
                if gchunk > 0:
                    xr1 = V(zrs_p, TC - 1, [[TC, 32]]); xi1 = V(zis_p, TC - 1, [[TC, 32]])
                    B_xrp, B_xip = B_zrsp, B_zisp
                    vtt(m1[:], B_m, Kr[:], B_tab, xr1, B_xrp, ALU.mult)
                    vtt(m2[:], B_m, Ki[:], B_tab, xi1, B_xip, ALU.mult)
                    vtt(m1[:], B_m, m1[:], B_m, m2[:], B_m, ALU.subtract)
                    vtt(zr[:, :, 0], B_zr, zr[:, :, 0], B_zr, m1[:], B_m, ALU.add)
                    vtt(m1[:], B_m, Kr[:], B_tab, xi1, B_xip, ALU.mult)
                    vtt(m2[:], B_m, Ki[:], B_tab, xr1, B_xrp, ALU.mult)
                    vtt(m1[:], B_m, m1[:], B_m, m2[:], B_m, ALU.add)
                    vtt(zi[:, :, 0], B_zi, zi[:, :, 0], B_zi, m1[:], B_m, ALU.add)
                kb.op("dve", [B_tab, B_zr], [B_zrs], lambda zrf=zrf: nc.vector.tensor_tensor_scan(
                    out=zrs[:], data0=rtf, data1=zrf, initial=0.0, op0=ALU.mult, op1=ALU.add))
                kb.op("dve", [B_tab, B_zi], [B_zis], lambda zif=zif: nc.vector.tensor_tensor_scan(
                    out=zis[:], data0=rtf, data1=zif, initial=0.0, op0=ALU.mult, op1=ALU.add))
                kb.op("act", [B_zrs], [B_zrb], lambda zrs=zrs: nc.scalar.copy(out=zrb[:], in_=zrs[:]))
                kb.op("act", [B_zis], [B_zib], lambda zis=zis: nc.scalar.copy(out=zib[:], in_=zis[:]))
                xrf = XRb[:].rearrange("p a b -> p (a b)"); xif = XIb[:].rearrange("p a b -> p (a b)")
                vtt(tC[:], B_tC, zrb[:], B_zrb, Ctb[:], B_tab, ALU.mult)
                vtt(tD[:], B_tD, zib[:], B_zib, Stb[:], B_tab, ALU.mult)
                vtt(xrf, B_XRb, tC[:], B_tC, tD[:], B_tD, ALU.subtract)
                vtt(tC[:], B_tC, zrb[:], B_zrb, Stb[:], B_tab, ALU.mult)
                vtt(tD[:], B_tD, zib[:], B_zib, Ctb[:], B_tab, ALU.mult)
                vtt(xif, B_XIb, tC[:], B_tC, tD[:], B_tD, ALU.add)
                kb.op("pe", [B_CL, B_XRb, B_XIb, B_Dg, B_ub], [B_psY], lambda s0=s0, ubt=ubt: [
                    mm for t in range(8) for mm in (
                        [nc.tensor.matmul(psY[:, t, :], lhsT=CL[:, 4 * t + jl, ri, :],
                                          rhs=(XRb, XIb)[ri][:, 4 * t + jl, :],
                                          start=(jl == 0 and ri == 0), stop=False)
                         for jl in range(4) for ri in range(2)] +
                        [nc.tensor.matmul(psY[:, t, :], lhsT=Dg[:, t, :], rhs=ubt[:, t, s0:s0 + TC],
                                          start=False, stop=True)])])
                kb.op("act", [B_psY], [B_yg], lambda ygt=ygt, s0=s0: nc.scalar.activation(
                    out=ygt[:, :, s0:s0 + TC], in_=psY[:], func=AF.Gelu_apprx_tanh))
                gchunk += 1
            for co in range(8):
                pz, B_pz = psZ[co % 2]
                kb.op("pe", [B_wg, B_yg], [B_pz], lambda co=co, pz=pz, ygt=ygt: [
                    nc.tensor.matmul(pz[:], lhsT=wg[:, kc, co * 128:(co + 1) * 128], rhs=ygt[:, kc, :],
                                     start=(kc == 0), stop=(kc == 7)) for kc in range(8)])
                kb.op("act", [B_pz, B_bg], [B_sig], lambda co=co, pz=pz: nc.scalar.activation(
                    out=sig[:], in_=pz[:], func=AF.Sigmoid, bias=bg[:, co:co + 1]))
                kb.op("dve", [B_sig, B_yg], [B_ysb], lambda co=co, ygt=ygt: nc.vector.tensor_tensor(
                    out=ysb[:, co, :], in0=ygt[:, co, :], in1=sig[:], op=ALU.mult))
            kb.dma("sp", B_ysb, [B_ysb], [B_ys], lambda blk=blk: nc.sync.dma_start(
                out=DV(ys_scr, blk * 512, [[L, 128], [128 * L, 8], [1, 512]]), in_=ysb[:]))
        allb = [B_BbT, B_CL, B_tab, B_Dg, B_bg, B_wg, B_tA, B_tB, B_tC, B_tD, B_zr, B_zi, B_XRb, B_XIb, B_m,
                B_sig, B_ysb, B_psY, B_zrb, B_zib] + [b for _, b in ub + yg + psBU + psZ + zrs2 + zis2]
        for en in ("pe", "dve", "act", "pool", "sp"):
            kb.wait_all(en, allb)


    def phase_c(pc):
        wo, B_wo = sb("wo", [128, 16, D], BF16, pc)
        load_w_bf16(wo, B_wo, w_out, 16, D)
        wcq, B_wcq = sb("wcq", [128, 16, 512], BF16, pc)
        load_w_bf16(wcq, B_wcq, w_cq, 16, 512)
        wco, B_wco = sb("wco", [128, 4, D], BF16, pc)
        load_w_bf16(wco, B_wco, w_co, 4, D)
        g_cr, B_gcr = sb("g_cr", [128, D], F32, pc)
        bcast_load("sp", g_cr, B_gcr, norm_cross, D)
        KTm, B_KTm = sb("KTm", [128, 4, 256], BF16, pc)
        Vm, B_Vm = sb("Vm", [128, 2, 4, 129], BF16, pc)
        ss, B_ss = sb("ssc", [128, 4], F32, pc)
        junk, B_junk = sb("junkc", [128, D], BF16, pc)
        pst, B_pst = ps("pstC", [128, 16, 128], BF16, pc)
        psH, B_psH = ps("psH", [128, D], F32, pc)
        psQ, B_psQ = ps("psQ", [128, 512], F32, pc)
        psSc, B_psSc = ps("psSc", [128, 512], F32, pc)
        with ExitStack() as s2:
            wkv, B_wkv = sb("wkv", [128, 16, 1024], BF16, s2)
            load_w_bf16(wkv, B_wkv, w_ckv, 16, 1024)
            g_me, B_gme = sb("g_me", [128, D], F32, s2)
            bcast_load("sp", g_me, B_gme, norm_mem, D)
            mt_, B_mt = sb("memt", [128, D], F32, s2)
            mn, B_mn = sb("memn", [128, D], BF16, s2)
            memT, B_memT = sb("memT", [128, 16, 256], BF16, s2)
            for mt in range(2):
                kb.dma("sp", B_mt, [], [B_mt], lambda mt=mt: nc.sync.dma_start(
                    out=mt_[:], in_=mem[mt * 128:(mt + 1) * 128, :]))
                rmsnorm_tile(mt_[:], B_mt, g_me, B_gme, [(mn[:], B_mn)], ss, B_ss, junk, B_junk)
                transposes(mn, B_mn, 16, pst, B_pst, memT[:, :, mt * 128:(mt + 1) * 128], B_memT)
            for h in range(4):
                kb.op("pe", [B_wkv, B_memT], [B_psQ], lambda h=h: [
                    nc.tensor.matmul(psQ[:, 0:256], lhsT=wkv[:, kc, h * 128:(h + 1) * 128], rhs=memT[:, kc, :],
                                     start=(kc == 0), stop=(kc == 15)) for kc in range(16)])
                kb.op("act", [B_psQ], [B_KTm], lambda h=h: nc.scalar.copy(out=KTm[:, h, :], in_=psQ[:, 0:256]))
            kb.op("dve", [], [B_Vm], lambda: nc.vector.memset(Vm[:, :, :, 128:129], 1.0))
            for mt in range(2):
                kb.op("pe", [B_wkv, B_memT], [B_psSc], lambda mt=mt: [
                    nc.tensor.matmul(psSc[:], lhsT=memT[:, kc, mt * 128:(mt + 1) * 128], rhs=wkv[:, kc, 512:1024],
                                     start=(kc == 0), stop=(kc == 15)) for kc in range(16)])
                kb.op("act", [B_psSc], [B_Vm], lambda mt=mt: nc.scalar.copy(
                    out=Vm[:, mt, :, 0:128], in_=psSc[:].rearrange("p (h d) -> p h d", h=4)))
            for en in ("pe", "dve", "act", "pool", "sp"):
                kb.wait_all(en, [B_wkv, B_gme, B_mt, B_mn, B_memT])

        xc = [sb(f"xc{i}", [128, D], F32, pc) for i in range(2)]
        yac = [sb(f"yac{i}", [128, 1024], BF16, pc) for i in range(2)]
        ysc = [sb(f"ysc{i}", [128, 8, 128], BF16, pc) for i in range(2)]
        yaT, B_yaT = sb("yaT", [128, 8, 128], BF16, pc)
        h1, B_h1 = sb("h1c", [128, D], F32, pc)
        hn, B_hn = sb("hnc", [128, D], BF16, pc)
        hnT, B_hnT = sb("hnTc", [128, 16, 128], BF16, pc)
        qTc, B_qTc = sb("qTc", [128, 4, 128], BF16, pc)
        pTc, B_pTc = sb("pTc", [128, 512], BF16, pc)
        rc, B_rc = sb("rcc", [128, 2], F32, pc)
        on, B_on = sb("onc", [128, 512], BF16, pc)
        onT, B_onT = sb("onTc", [128, 4, 128], BF16, pc)
        h2t, B_h2t = sb("h2c", [128, D], F32, pc)

        def load_c(i):
            xt_, B_x_ = xc[i % 2]; ya_, B_ya_ = yac[i % 2]; ys_, B_ys_ = ysc[i % 2]
            kb.dma("sp", B_x_, [], [B_x_], lambda: nc.sync.dma_start(out=xt_[:], in_=x[i * 128:(i + 1) * 128, :]))
            kb.dma("sp", B_ya_, [B_ya], [B_ya_], lambda: nc.sync.dma_start(
                out=ya_[:], in_=ya_scr[i * 128:(i + 1) * 128, :]))
            kb.dma("sp", B_ys_, [B_ys], [B_ys_], lambda: nc.sync.dma_start(
                out=ys_[:], in_=DV(ys_scr, i * 128, [[L, 128], [128 * L, 8], [1, 128]])))

        load_c(0)
        for i in range(NT):
            if i + 1 < NT:
                load_c(i + 1)
            xt, B_x = xc[i % 2]; yat, B_yat = yac[i % 2]; yst, B_yst = ysc[i % 2]
            transposes(yat, B_yat, 8, pst, B_pst, yaT, B_yaT)
            kb.op("pe", [B_wo, B_yst, B_yaT], [B_psH], lambda yst=yst: [
                nc.tensor.matmul(psH[:, nb * 512:(nb + 1) * 512],
                                 lhsT=(yst[:, kc, :] if kc < 8 else yaT[:, kc - 8, :]),
                                 rhs=wo[:, kc, nb * 512:(nb + 1) * 512], start=(kc == 0), stop=(kc == 15))
                for nb in range(4) for kc in range(16)])
            kb.op("dve", [B_psH, B_x], [B_h1], lambda xt=xt: nc.vector.tensor_tensor(
                out=h1[:], in0=psH[:], in1=xt[:], op=ALU.add))
            rmsnorm_tile(h1[:], B_h1, g_cr, B_gcr, [(hn[:], B_hn)], ss, B_ss, junk, B_junk)
            transposes(hn, B_hn, 16, pst, B_pst, hnT, B_hnT)
            kb.op("pe", [B_wcq, B_hnT], [B_psQ], lambda: [
                nc.tensor.matmul(psQ[:, h * 128:(h + 1) * 128], lhsT=wcq[:, kc, h * 128:(h + 1) * 128],
                                 rhs=hnT[:, kc, :], start=(kc == 0), stop=(kc == 15))
                for h in range(4) for kc in range(16)])
            kb.op("act", [B_psQ], [B_qTc], lambda: nc.scalar.copy(
                out=qTc[:].rearrange("p a b -> p (a b)"), in_=psQ[:]))
            for hh in range(2):
                kb.op("pe", [B_KTm, B_qTc], [B_psSc], lambda hh=hh: [
                    nc.tensor.matmul(psSc[:, (mt * 2 + hl) * 128:(mt * 2 + hl + 1) * 128],
                                     lhsT=KTm[:, 2 * hh + hl, mt * 128:(mt + 1) * 128], rhs=qTc[:, 2 * hh + hl, :],
                                     start=True, stop=True) for mt in range(2) for hl in range(2)])
                kb.op("act", [B_psSc], [B_pTc], lambda: nc.scalar.activation(
                    out=pTc[:], in_=psSc[:], func=AF.Exp, scale=128.0 ** -0.5))
                kb.op("pe", [B_pTc, B_Vm], [B_psQ], lambda hh=hh: [
                    nc.tensor.matmul(psQ[:, hl * 256:hl * 256 + 129],
                                     lhsT=pTc[:, (mt * 2 + hl) * 128:(mt * 2 + hl + 1) * 128],
                                     rhs=Vm[:, mt, 2 * hh + hl, :], start=(mt == 0), stop=(mt == 1))
                    for hl in range(2) for mt in range(2)])
                kb.op("dve", [B_psQ], [B_rc], lambda: nc.vector.reciprocal(
                    out=rc[:], in_=V(psQ, 128, [[256, 2]])))
                for hl in range(2):
                    kb.op("dve", [B_psQ, B_rc], [B_on], lambda hh=hh, hl=hl: nc.vector.tensor_scalar(
                        out=on[:, (2 * hh + hl) * 128:(2 * hh + hl + 1) * 128], in0=psQ[:, hl * 256:hl * 256 + 128],
                        scalar1=rc[:, hl:hl + 1], scalar2=None, op0=ALU.mult))
            transposes(on, B_on, 4, pst, B_pst, onT, B_onT)
            kb.op("pe", [B_wco, B_onT], [B_psH], lambda: [
                nc.tensor.matmul(psH[:, nb * 512:(nb + 1) * 512], lhsT=onT[:, kc, :],
                                 rhs=wco[:, kc, nb * 512:(nb + 1) * 512], start=(kc == 0), stop=(kc == 3))
                for nb in range(4) for kc in range(4)])
            kb.op("dve", [B_psH, B_h1], [B_h2t], lambda: nc.vector.tensor_tensor(
                out=h2t[:], in0=psH[:], in1=h1[:], op=ALU.add))
            kb.dma("sp", B_h2t, [B_h2t], [B_h2], lambda i=i: nc.sync.dma_start(
                out=h2_scr[i * 128:(i + 1) * 128, :], in_=h2t[:]))
        allb = [B_wo, B_wcq, B_wco, B_gcr, B_KTm, B_Vm, B_ss, B_junk, B_pst, B_psH, B_psQ, B_psSc, B_yaT, B_h1,
                B_hn, B_hnT, B_qTc, B_pTc, B_rc, B_on, B_onT, B_h2t] + [b for _, b in xc + yac + ysc]
        for en in ("pe", "dve", "act", "pool", "sp"):
            kb.wait_all(en, allb)


    import os
    NG = int(os.environ.get('PEER_NG', '6'))
    SKIP_DVE = os.environ.get('PEER_SKIP_DVE') == '1'
    SKIP_DMA = os.environ.get('PEER_SKIP_DMA') == '1'
    def phase_d(pd):
        wpq, B_wpq = sb("wpq", [128, 16, D], BF16, pd)
        load_w_bf16(wpq, B_wpq, w_pq, 16, D)
        skT, B_skT = sb("skT", [128, 16, 128], BF16, pd)
        g_ff, B_gff = sb("g_ff", [128, D], F32, pd)
        bcast_load("sp", g_ff, B_gff, norm_ffn, D)
        g_fi, B_gfi = sb("g_fi", [128, D], F32, pd)
        bcast_load("sp", g_fi, B_gfi, norm_final, D)
        io16, B_io = sb("io16", [128, 16], F32, pd)
        bcast_load("sp", io16, B_io, iota16, 16)
        ss, B_ss = sb("ssd", [128, 4], F32, pd)
        junk, B_junk = sb("junkd", [128, D], BF16, pd)
        pst, B_pst = ps("pstD", [128, 16, 128], BF16, pd)
        psG, B_psG = ps("psG", [128, D], F32, pd)
        W = [sb(f"Wd{i}", [128, D], F32, pd) for i in range(3)]
        W.append(W[1])
        if "B" not in phases:
            convert_tables()
        kb.wait_all("pool", cvb)
        kb.dma("sp", W[0][1], [], [W[0][1]], lambda: nc.sync.dma_start(
            out=W[0][0][:].rearrange("p (a b) -> p a b", a=16), in_=DV(sub_keys, 0, [[128, 128], [16384, 16], [1, 128]])))
        for q4 in range(4):
            kb.op("pe", [W[0][1], B_idf], [B_psG], lambda q4=q4: [
                nc.tensor.transpose(out=psG[:, (4 * q4 + jj) * 128:(4 * q4 + jj + 1) * 128],
                                    in_=W[0][0][:, (4 * q4 + jj) * 128:(4 * q4 + jj + 1) * 128], identity=ident_f[:])
                for jj in range(4)])
        kb.op("act", [B_psG], [B_skT], lambda: nc.scalar.copy(out=skT[:].rearrange("p a b -> p (a b)"), in_=psG[:]))

        h2ts = [sb(f"h2d{i}", [128, D], F32, pd) for i in range(2)]
        idxs = [sb(f"idxd{i}", [128, 128], I32, pd) for i in range(2)]
        ss2, B_ss2 = sb("ssd2", [128, 4], F32, pd)
        junk2, B_junk2 = sb("junkd2", [128, D], BF16, pd)
        psF, B_psF = ps("psF", [128, 1024], F32, pd)
        hnf, B_hnf = sb("hnfd", [128, D], F32, pd)
        hnb, B_hnb = sb("hnbd", [128, D], BF16, pd)
        hnT, B_hnT = sb("hnTd", [128, 16, 128], BF16, pd)
        qTp, B_qTp = sb("qTpd", [128, 16, 128], BF16, pd)
        stp, B_stp = sb("stopd", [128, 16, 16], F32, pd)
        itp, B_itp = sb("itopd", [128, 16, 16], U32, pd)
        itf, B_itf = sb("itfd", [128, 16, 16], F32, pd)
        best, B_best = sb("bestd", [128, 8, 16], F32, pd)
        pos, B_pos = sb("posd", [128, 8, 16], U32, pd)
        ai, B_ai = sb("aid", [128, 128], U32, pd)
        af, B_af = sb("afd", [128, 128], F32, pd)
        bf_, B_bf = sb("bfd", [128, 128], F32, pd)
        i1g, B_i1g = sb("i1gd", [128, 128], F32, pd)
        i2g, B_i2g = sb("i2gd", [128, 128], F32, pd)
        ebs = [sb(f"ebd{i}", [128, 8, 16], F32, pd) for i in range(2)]
        sm, B_sm = sb("smd", [128, 16], F32, pd)
        actv, B_actv = sb("actd", [128, 128], F32, pd)
        coef, B_coef = sb("coefd", [128, 128], F32, pd)
        acc, B_acc = sb("accd", [128, D], F32, pd)
        gb = [sb(f"gbd{i}", [128, D], BF16, pd) for i in range(NG)]
        dgs = [sb(f"dgd{i}", [128, 16, 128], BF16, pd) for i in range(2)]
        gcount = [0]
        (s_sb, B_s), (s2_, B_s2), (cand, B_cand), (c2, B_c2) = W

        def front(i):
            h2t, B_h2t = h2ts[i % 2]
            idx, B_idx = idxs[i % 2]
            eb, B_eb = ebs[i % 2]
            kb.dma("sp", B_h2t, [B_h2], [B_h2t], lambda: nc.sync.dma_start(
                out=h2t[:], in_=h2_scr[i * 128:(i + 1) * 128, :]))
            rmsnorm_tile(h2t[:], B_h2t, g_ff, B_gff, [(hnf[:], B_hnf), (hnb[:], B_hnb)], ss, B_ss, junk2, B_junk2)
            yield
            transposes(hnb, B_hnb, 16, pst, B_pst, hnT, B_hnT)
            yield
            for half in range(2):
                kb.op("pe", [B_wpq, B_hnT], [B_psF], lambda half=half: [
                    nc.tensor.matmul(psF[:, hl * 128:(hl + 1) * 128],
                                     lhsT=wpq[:, kc, (8 * half + hl) * 128:(8 * half + hl + 1) * 128],
                                     rhs=hnT[:, kc, :], start=(kc == 0), stop=(kc == 15))
                    for hl in range(8) for kc in range(16)])
                kb.op("act", [B_psF], [B_qTp], lambda half=half: nc.scalar.copy(
                    out=qTp[:, 8 * half:8 * half + 8, :].rearrange("p a b -> p (a b)"), in_=psF[:]))
                yield
            for half in range(2):
                kb.op("pe", [B_qTp, B_skT], [B_psF], lambda half=half: [
                    nc.tensor.matmul(psF[:, hl * 128:(hl + 1) * 128], lhsT=qTp[:, 8 * half + hl, :],
                                     rhs=skT[:, 8 * half + hl, :], start=True, stop=True) for hl in range(8)])
                kb.op("act", [B_psF], [B_s], lambda half=half: nc.scalar.copy(
                    out=s_sb[:, half * 1024:(half + 1) * 1024], in_=psF[:]))
                yield
            for hc in range(16):
                sv = s_sb[:, hc * 128:(hc + 1) * 128]
                s2v = s2_[:, hc * 128:(hc + 1) * 128]
                kb.op("dve", [B_s], [B_stp], lambda hc=hc, sv=sv: nc.vector.max(out=stp[:, hc, 0:8], in_=sv))
                kb.op("dve", [B_s, B_stp], [B_itp], lambda hc=hc, sv=sv: nc.vector.max_index(
                    out=itp[:, hc, 0:8], in_max=stp[:, hc, 0:8], in_values=sv))
                kb.op("dve", [B_s, B_stp], [B_s2], lambda hc=hc, sv=sv, s2v=s2v: nc.vector.match_replace(
                    out=s2v, in_to_replace=stp[:, hc, 0:8], in_values=sv, imm_value=-1e30))
                kb.op("dve", [B_s2], [B_stp], lambda hc=hc, s2v=s2v: nc.vector.max(out=stp[:, hc, 8:16], in_=s2v))
                kb.op("dve", [B_s2, B_stp], [B_itp], lambda hc=hc, s2v=s2v: nc.vector.max_index(
                    out=itp[:, hc, 8:16], in_max=stp[:, hc, 8:16], in_values=s2v))
                yield
            kb.op("dve", [B_stp], [B_cand], lambda: nc.vector.tensor_tensor(
                out=cand[:].rearrange("p (h a b) -> p h a b", h=8, a=16),
                in0=V(stp, 0, [[32, 8], [1, 16], [0, 16]]), in1=V(stp, 16, [[32, 8], [0, 16], [1, 16]]), op=ALU.add))
            yield
            for h in range(8):
                cv = cand[:, h * 256:(h + 1) * 256]
                c2v = c2[:, h * 256:(h + 1) * 256]
                kb.op("dve", [B_cand], [B_best], lambda h=h, cv=cv: nc.vector.max(out=best[:, h, 0:8], in_=cv))
                kb.op("dve", [B_cand, B_best], [B_pos], lambda h=h, cv=cv: nc.vector.max_index(
                    out=pos[:, h, 0:8], in_max=best[:, h, 0:8], in_values=cv))
                kb.op("dve", [B_cand, B_best], [B_c2], lambda h=h, cv=cv, c2v=c2v: nc.vector.match_replace(
                    out=c2v, in_to_replace=best[:, h, 0:8], in_values=cv, imm_value=-1e30))
                kb.op("dve", [B_c2], [B_best], lambda h=h, c2v=c2v: nc.vector.max(out=best[:, h, 8:16], in_=c2v))
                kb.op("dve", [B_c2, B_best], [B_pos], lambda h=h, c2v=c2v: nc.vector.max_index(
                    out=pos[:, h, 8:16], in_max=best[:, h, 8:16], in_values=c2v))
                yield
            posf = pos[:].rearrange("p a b -> p (a b)")
            kb.op("dve", [B_itp], [B_itf], lambda: nc.vector.tensor_copy(out=itf[:], in_=itp[:]))
            kb.op("dve", [B_pos], [B_ai], lambda: nc.vector.tensor_single_scalar(
                out=ai[:], in_=posf, scalar=4, op=ALU.logical_shift_right))
            kb.op("dve", [B_ai], [B_af], lambda: nc.vector.tensor_copy(out=af[:], in_=ai[:]))
            kb.op("dve", [B_pos], [B_ai], lambda: nc.vector.tensor_single_scalar(
                out=ai[:], in_=posf, scalar=15, op=ALU.bitwise_and))
            kb.op("dve", [B_ai], [B_bf], lambda: nc.vector.tensor_copy(out=bf_[:], in_=ai[:]))
            yield
            for (sel, B_sel, c_off, og, B_og) in ((af, B_af, 0, i1g, B_i1g), (bf_, B_bf, 16, i2g, B_i2g)):
                kb.op("dve", [B_sel, B_io], [B_s], lambda sel=sel: nc.vector.tensor_tensor(
                    out=s_sb[:].rearrange("p (h k a) -> p h k a", h=8, k=16),
                    in0=V(sel, 0, [[16, 8], [1, 16], [0, 16]]), in1=V(io16, 0, [[0, 8], [0, 16], [1, 16]]),
                    op=ALU.is_equal))
                kb.op("dve", [B_s, B_itf], [B_s2], lambda c_off=c_off: nc.vector.tensor_tensor(
                    out=s2_[:].rearrange("p (h k a) -> p h k a", h=8, k=16),
                    in0=s_sb[:].rearrange("p (h k a) -> p h k a", h=8, k=16),
                    in1=V(itf, c_off, [[32, 8], [0, 16], [1, 16]]), op=ALU.mult))
                kb.op("dve", [B_s2], [B_og], lambda og=og: nc.vector.tensor_reduce(
                    out=og[:], in_=s2_[:].rearrange("p (n a) -> p n a", a=16), axis=AX.X, op=ALU.add))
                yield
            kb.op("dve", [B_i1g, B_i2g], [B_i1g], lambda: nc.vector.scalar_tensor_tensor(
                out=i1g[:], in0=i1g[:], scalar=128.0, in1=i2g[:], op0=ALU.mult, op1=ALU.add))
            kb.op("dve", [B_i1g], [B_idx], lambda: nc.vector.tensor_copy(out=idx[:], in_=i1g[:]))
            kb.op("dve", [B_best], [B_eb], lambda: nc.vector.tensor_tensor(
                out=eb[:], in0=best[:], in1=V(best, 0, [[16, 8], [0, 16]]), op=ALU.subtract))
            kb.op("act", [B_eb], [B_eb], lambda: nc.scalar.activation(out=eb[:], in_=eb[:], func=AF.Exp))
            kb.op("dve", [B_eb], [B_sm], lambda: nc.vector.tensor_reduce(
                out=sm[:, 0:8], in_=eb[:], axis=AX.X, op=ALU.add))
            kb.op("dve", [B_sm], [B_sm], lambda: nc.vector.reciprocal(out=sm[:, 8:16], in_=sm[:, 0:8]))
            kb.op("dve", [B_eb, B_sm], [B_eb], lambda: nc.vector.tensor_tensor(
                out=eb[:], in0=eb[:], in1=V(sm, 8, [[1, 8], [0, 16]]), op=ALU.mult))
            yield

        def useg(i):
            idx, B_idx = idxs[i % 2]
            eb, B_eb = ebs[i % 2]
            for hk in range(128):
                g_, B_g = gb[gcount[0] % NG]
                gcount[0] += 1
                kb.dma("pool", B_g, [B_idx], [B_g], lambda hk=hk, g_=g_: nc.gpsimd.indirect_dma_start(
                    out=g_[:], out_offset=None, in_=u_bf[:, :],
                    in_offset=bass.IndirectOffsetOnAxis(ap=idx[:, hk:hk + 1], axis=0)))
                kb.op("dve", [B_g, B_hnf], [B_junk, B_actv], lambda hk=hk, g_=g_: nc.vector.scalar_tensor_tensor(
                    out=junk[:], in0=g_[:], scalar=1.0, in1=hnf[:], op0=ALU.mult, op1=ALU.mult,
                    accum_out=actv[:, hk:hk + 1]))
            kb.op("act", [B_actv], [B_actv], lambda: nc.scalar.activation(
                out=actv[:], in_=actv[:], func=AF.Gelu_apprx_tanh))
            kb.op("dve", [B_actv, B_eb], [B_coef], lambda: nc.vector.tensor_tensor(
                out=coef[:], in0=actv[:], in1=eb[:].rearrange("p a b -> p (a b)"), op=ALU.mult))

        def vseg(i, gen):
            h2t, B_h2t = h2ts[i % 2]
            idx, B_idx = idxs[i % 2]
            for hk in range(128):
                dg_, B_dg = dgs[(hk // 16) % 2]
                if hk % 16 == 0:
                    kb.op("dve", [B_coef, B_idf], [B_dg], lambda hk=hk, dg_=dg_: nc.vector.tensor_tensor(
                        out=dg_[:], in0=V(ident_f, 0, [[0, 16], [1, 128]]), in1=V(coef, hk, [[1, 16], [0, 128]]),
                        op=ALU.mult))
                    if gen is not None:
                        for _ in range(6):
                            next(gen, None)
                g_, B_g = gb[gcount[0] % NG]
                gcount[0] += 1
                kb.dma("pool", B_g, [B_idx], [B_g], lambda hk=hk, g_=g_: nc.gpsimd.indirect_dma_start(
                    out=g_[:], out_offset=None, in_=v_bf[:, :],
                    in_offset=bass.IndirectOffsetOnAxis(ap=idx[:, hk:hk + 1], axis=0)))
                kb.op("pe", [B_g, B_dg], [B_psG], lambda hk=hk, g_=g_, dg_=dg_: [
                    nc.tensor.matmul(psG[:, nb * 512:(nb + 1) * 512], lhsT=dg_[:, hk % 16, :],
                                     rhs=g_[:, nb * 512:(nb + 1) * 512], start=(hk == 0), stop=(hk == 127))
                    for nb in range(4)])
            if gen is not None:
                for _ in gen:
                    pass
            kb.op("dve", [B_psG, B_h2t], [B_acc], lambda: nc.vector.tensor_tensor(
                out=acc[:], in0=psG[:], in1=h2t[:], op=ALU.add))
            rmsnorm_tile(acc[:], B_acc, g_fi, B_gfi, [(acc[:], B_acc)], ss2, B_ss2, junk2, B_junk2)
            kb.dma("sp", B_acc, [B_acc], [B_out], lambda: nc.sync.dma_start(
                out=out[i * 128:(i + 1) * 128, :], in_=acc[:]))

        for _ in front(0):
            pass
        for i in range(NT):
            useg(i)
            vseg(i, front(i + 1) if i + 1 < NT else None)
        allb = [B_wpq, B_skT, B_gff, B_gfi, B_io, B_ss, B_ss2, B_junk, B_junk2, B_pst, B_psG, B_psF, B_hnf, B_hnb,
                B_hnT, B_qTp, B_stp, B_itp, B_itf, B_best, B_pos, B_ai, B_af, B_bf, B_i1g, B_i2g, B_sm, B_actv,
                B_coef, B_acc] + [b for _, b in W[:3] + gb + dgs + h2ts + idxs + ebs]
        for en in ("pe", "dve", "act", "pool", "sp"):
            kb.wait_all(en, allb)

    for nm, fn in (("A", phase_a), ("B", phase_b), ("C", phase_c), ("D", phase_d)):
        if nm in phases:
            with ExitStack() as pstack:
                fn(pstack)
    dbg["ya"] = ya_scr
    dbg["u"] = u_scr
    kb.wait_all("sp", [B_u, B_ya, B_ys, B_h2, B_out])
    es.close()
    return nc, dbg


def host_consts():
    W = 128
    qi = np.arange(W)[:, None]
    kj = np.arange(2 * W)[None, :]
    dist = qi + W - kj
    inwin = (dist >= 0) & (dist < W)
    dc = np.clip(dist, 0, W - 1)
    max_exact = 16
    d_f = np.maximum(dc, 1).astype(np.float32)
    large = max_exact + (np.log(d_f / max_exact) / math.log(128 / max_exact) * (32 - max_exact)).astype(np.int32)
    large = np.minimum(large, 31)
    bucket = np.where(dc < max_exact, dc, large)
    E = np.zeros((128, 32, 2, 128), np.float32)
    neg = np.zeros((128, 2, 128), np.float32)
    for kb_ in range(2):
        for k in range(128):
            kjj = kb_ * 128 + k
            for b in range(32):
                E[k, b, kb_, :] = ((bucket[:, kjj] == b) & inwin[:, kjj]).astype(np.float32)
            neg[k, kb_, :] = np.where(inwin[:, kjj], 0.0, -30000.0)
    return E.reshape(128, -1), neg.reshape(128, -1), np.eye(128, dtype=np.float32)


def make_in_maps(inp, NT=32, cores=NCORES):
    L = NT * 128
    E, neg, ident = host_consts()
    w_in = inp["w_in"][0]
    kcols = w_in[:, 2048:2304].reshape(D, 4, 1, 64)
    k2 = np.broadcast_to(kcols, (D, 4, 2, 64)).reshape(D, 512)
    w_in_l = np.ascontiguousarray(np.concatenate([w_in[:, :2048], k2, w_in[:, 2304:2560]], axis=1))
    shared = {
        "rel_bias": inp["rel_bias"], "norm_mix": inp["norm_mix"], "w_in": w_in_l,
        "lam_re": inp["ssm_lambda_re"][0], "lam_im": inp["ssm_lambda_im"][0],
        "b_re": inp["ssm_b_re"][0], "b_im": inp["ssm_b_im"][0],
        "c_re": inp["ssm_c_re"][0], "c_im": inp["ssm_c_im"][0],
        "ssm_d": inp["ssm_d"].reshape(1, 1024), "log_dt": inp["ssm_log_dt"],
        "w_glu": inp["ssm_w_glu"][0], "b_glu": inp["ssm_b_glu"], "sinks": inp["attn_sinks"],
        "w_out": inp["w_out"][0], "norm_cross": inp["norm_cross"], "norm_mem": inp["norm_mem"],
        "w_cq": inp["w_cq"][0], "w_ckv": inp["w_ckv"][0], "w_co": inp["w_co"][0],
        "norm_ffn": inp["norm_ffn"], "w_pq": inp["peer_w_q"][0],
        "sub_keys": inp["peer_sub_keys"][0].reshape(16, 128, 128),
        "peer_u": inp["peer_u"][0], "peer_v": inp["peer_v"][0],
        "norm_final": inp["norm_final"].reshape(1, D),
        "ebias": E, "negmask": neg, "ident": ident, "iota16": np.arange(16, dtype=np.float32).reshape(1, 16),
    }
    shared = {k: np.ascontiguousarray(np.asarray(v, dtype=np.float32)) for k, v in shared.items()}
    maps = []
    for c in range(cores):
        m = dict(shared)
        m["x"] = np.ascontiguousarray(np.asarray(inp["x"][c, :L], dtype=np.float32))
        m["mem"] = np.ascontiguousarray(np.asarray(inp["mem"][c], dtype=np.float32))
        maps.append(m)
    return maps


def kernel(**inputs):
    nc, _ = build(32)
    maps = make_in_maps(inputs, 32, NCORES)
    res = run_bass_kernel_spmd(nc, maps, core_ids=list(range(NCORES)))
    return np.stack([np.asarray(r["out"], dtype=np.float32) for r in res.results], axis=0)
```

```python
import math
from contextlib import ExitStack
import numpy as np
import concourse.bass as bass
import concourse.mybir as mybir
from concourse.bass_utils import run_bass_kernel_spmd

F32 = mybir.dt.float32
BF16 = mybir.dt.bfloat16
U32 = mybir.dt.uint32
I32 = mybir.dt.int32
AF = mybir.ActivationFunctionType
ALU = mybir.AluOpType
AX = mybir.AxisListType

D = 2048
SEQ = 4096
NCORES = 8
EPS = 1e-6
TC = 32
SAME_ENGINE_SYNC = True


class Buf:
    __slots__ = ("name", "w", "r", "dsem", "dcnt")

    def __init__(self, name):
        self.name = name
        self.w = None
        self.r = []
        self.dsem = None
        self.dcnt = 0


class Eng:
    def __init__(self, name, h, sem):
        self.name = name
        self.h = h
        self.sem = sem
        self.cnt = 0
        self.waited = {}


class KB:
    def __init__(self, nc, es):
        self.nc = nc
        self.es = es
        self.eng = {}
        for name, h in (("pe", nc.tensor), ("dve", nc.vector), ("act", nc.scalar),
                        ("pool", nc.gpsimd), ("sp", nc.sync)):
            self.eng[name] = Eng(name, h, es.enter_context(nc.semaphore("sem_" + name)))
        self.nsem = 5

    def _wait(self, e, tok, raw=True):
        if tok is None:
            return
        sem, val = tok
        if sem is e.sem and (e.name == "pe" or not SAME_ENGINE_SYNC or not raw):
            return
        k = id(sem)
        if e.waited.get(k, 0) >= val:
            return
        e.h.wait_ge(sem, val)
        e.waited[k] = val

    def _deps(self, e, reads, writes):
        for b in reads:
            self._wait(e, b.w)
        for b in writes:
            self._wait(e, b.w)
            for t in b.r:
                self._wait(e, t)

    def _commit(self, tok, reads, writes):
        for b in reads:
            b.r = [t for t in b.r if t[0] is not tok[0]] + [tok]
        for b in writes:
            b.w = tok
            b.r = []

    def op(self, ename, reads, writes, fn):
        e = self.eng[ename]
        self._deps(e, reads, writes)
        ins = fn()
        if isinstance(ins, (list, tuple)):
            ins = ins[-1]
        e.cnt += 1
        ins.then_inc(e.sem, 1)
        self._commit((e.sem, e.cnt), reads, writes)

    def dma(self, qname, sbuf_buf, reads, writes, fn):
        e = self.eng[qname]
        self._deps(e, reads, writes)
        if sbuf_buf.dsem is None:
            sbuf_buf.dsem = self.es.enter_context(self.nc.semaphore("d_" + sbuf_buf.name))
            self.nsem += 1
        ins = fn()
        sbuf_buf.dcnt += 16
        ins.then_inc(sbuf_buf.dsem, 16)
        self._commit((sbuf_buf.dsem, sbuf_buf.dcnt), reads, writes)

    def wait_all(self, ename, bufs):
        e = self.eng[ename]
        for b in bufs:
            self._wait(e, b.w)
            for t in b.r:
                self._wait(e, t)


def V(t, off, dims, p0=0, npart=128):
    a = t[:]
    ps = a.ap[0][0]
    return bass.AP(tensor=a.tensor, offset=a.offset + p0 * ps + off,
                   ap=[[ps, npart]] + [list(d) for d in dims])


def DV(d, off, dims):
    return bass.AP(tensor=d.tensor, offset=d.offset + off, ap=[list(x) for x in dims])


def build(NT=32, debug=False, stop=99, phases="ABCD"):
    L = NT * 128
    nc = bass.Bass("TRN2", target_bir_lowering=False)
    es = ExitStack()
    kb = KB(nc, es)

    def din(name, shape, dt=F32):
        return nc.dram_tensor(name, list(shape), dt, kind="ExternalInput").ap()

    def dscr(name, shape, dt):
        return nc.dram_tensor(name, list(shape), dt, kind="Internal").ap()

    x = din("x", [L, D])
    mem = din("mem", [256, D])
    rel_bias = din("rel_bias", [32, 16])
    norm_mix = din("norm_mix", [1, D])
    w_in = din("w_in", [D, 2816])
    lam_re = din("lam_re", [64, 64])
    lam_im = din("lam_im", [64, 64])
    b_re = din("b_re", [64, 64, 16])
    b_im = din("b_im", [64, 64, 16])
    c_re = din("c_re", [64, 16, 64])
    c_im = din("c_im", [64, 16, 64])
    ssm_d = din("ssm_d", [1, 1024])
    log_dt = din("log_dt", [1, 64])
    w_glu = din("w_glu", [1024, 1024])
    b_glu = din("b_glu", [1, 1024])
    sinks = din("sinks", [1, 16])
    w_out = din("w_out", [D, D])
    norm_cross = din("norm_cross", [1, D])
    norm_mem = din("norm_mem", [1, D])
    w_cq = din("w_cq", [D, 512])
    w_ckv = din("w_ckv", [D, 1024])
    w_co = din("w_co", [512, D])
    norm_ffn = din("norm_ffn", [1, D])
    w_pq = din("w_pq", [D, D])
    sub_keys = din("sub_keys", [16, 128, 128])
    peer_u = din("peer_u", [16384, D])
    peer_v = din("peer_v", [16384, D])
    norm_final = din("norm_final", [1, D])
    ebias = din("ebias", [128, 32 * 256])
    negmask = din("negmask", [128, 256])
    ident_d = din("ident", [128, 128])
    iota16 = din("iota16", [1, 16])
    out = nc.dram_tensor("out", [L, D], F32, kind="ExternalOutput").ap()

    u_scr = dscr("u_scr", [1024, L], BF16)
    ya_scr = dscr("ya_scr", [L, 1024], BF16)
    ys_scr = dscr("ys_scr", [1024, L], BF16)
    h2_scr = dscr("h2_scr", [L, D], F32)
    u_bf = dscr("u_bf", [16384, D], BF16)
    v_bf = dscr("v_bf", [16384, D], BF16)
    cvb = [Buf("cv0"), Buf("cv1")]
    B_u = Buf("u_scr"); B_ya = Buf("ya_scr"); B_ys = Buf("ys_scr"); B_h2 = Buf("h2_scr")
    B_out = Buf("out")
    dbg = {}
    if debug:
        dbg["h2"] = h2_scr

    def sb(name, shape, dt, stack=None):
        t = (stack or es).enter_context(nc.sbuf_tensor(name, list(shape), dt))
        return t, Buf(name)

    def ps(name, shape, dt, stack=None):
        t = (stack or es).enter_context(nc.psum_tensor(name, list(shape), dt))
        return t, Buf(name)

    def bcast_load(q, t, tb, src, n):
        kb.dma(q, tb, [], [tb], lambda: kb.eng[q].h.dma_start(
            out=t[:], in_=DV(src, 0, [[0, 128], [1, n]])))

    ident_f, B_idf = sb("ident_f", [128, 128], F32)
    ident_b, B_idb = sb("ident_b", [128, 128], BF16)
    kb.dma("sp", B_idf, [], [B_idf], lambda: nc.sync.dma_start(out=ident_f[:], in_=ident_d[:, :]))
    kb.op("dve", [B_idf], [B_idb], lambda: nc.vector.tensor_copy(out=ident_b[:], in_=ident_f[:]))

    def rmsnorm_tile(xt, B_x, g_bc, B_g, outs, ss, B_ss, junk, B_junk):
        kb.op("act", [B_x], [B_junk, B_ss], lambda: nc.scalar.activation(
            out=junk[:], in_=xt, func=AF.Square, accum_out=ss[:, 0:1]))
        kb.op("act", [B_ss], [B_ss], lambda: nc.scalar.activation(
            out=ss[:, 1:2], in_=ss[:, 0:1], func=AF.Sqrt, scale=1.0 / D, bias=EPS))
        kb.op("dve", [B_ss], [B_ss], lambda: nc.vector.reciprocal(out=ss[:, 2:3], in_=ss[:, 1:2]))
        for o_ap, B_o in outs:
            kb.op("dve", [B_x, B_ss, B_g], [B_o], lambda o_ap=o_ap: nc.vector.scalar_tensor_tensor(
                out=o_ap, in0=xt, scalar=ss[:, 2:3], in1=g_bc[:], op0=ALU.mult, op1=ALU.mult))

    def transposes(src_t, B_src, n, pst, B_pst, dst, B_dst, evac="act"):
        kb.op("pe", [B_src, B_idb], [B_pst], lambda: [
            nc.tensor.transpose(out=pst[:, k, :], in_=src_t[:, k * 128:(k + 1) * 128], identity=ident_b[:])
            for k in range(n)])
        dst_ap = dst if isinstance(dst, bass.AP) else dst[:, 0:n, :]
        if evac == "act":
            kb.op("act", [B_pst], [B_dst], lambda: nc.scalar.copy(out=dst_ap, in_=pst[:, 0:n, :]))
        else:
            kb.op("dve", [B_pst], [B_dst], lambda: nc.vector.tensor_copy(out=dst_ap, in_=pst[:, 0:n, :]))

    def load_w_bf16(t, B_t, src, nk, ncols):
        for c0 in range(0, ncols, 2048):
            c1 = min(ncols, c0 + 2048)
            for k0 in range(0, nk, 4):
                k1 = min(nk, k0 + 4)
                kb.dma("pool", B_t, [], [B_t], lambda c0=c0, c1=c1, k0=k0, k1=k1: nc.gpsimd.dma_start(
                    out=t[:, k0:k1, c0:c1],
                    in_=DV(src, k0 * 128 * ncols + c0, [[ncols, 128], [128 * ncols, k1 - k0], [1, c1 - c0]])))

    def convert_tables():
        k = 0
        for src, dst in ((peer_u, u_bf), (peer_v, v_bf)):
            for r0 in range(0, 16384, 4096):
                c = cvb[k % 2]
                k += 1
                kb.dma("pool", c, [], [c], lambda src=src, dst=dst, r0=r0: nc.gpsimd.dma_start(
                    out=dst[r0:r0 + 4096, :], in_=src[r0:r0 + 4096, :]))

    def phase_a(pa):
        w_in_sb, B_win = sb("w_in_sb", [128, 16, 2816], BF16, pa)
        load_w_bf16(w_in_sb, B_win, w_in, 16, 2816)
        g_mix, B_gmix = sb("g_mix", [128, D], F32, pa)
        bcast_load("sp", g_mix, B_gmix, norm_mix, D)
        biasT, B_bias = sb("biasT", [128, 16, 256], F32, pa)
        esink, B_esink = sb("esink", [128, 16], F32, pa)
        bcast_load("sp", esink, B_esink, sinks, 16)
        kb.op("act", [B_esink], [B_esink], lambda: nc.scalar.activation(out=esink[:], in_=esink[:], func=AF.Exp))
        with ExitStack() as s0:
            E_sb, B_E = sb("E_sb", [128, 32 * 256], F32, s0)
            rb, B_rb = sb("rb", [128, 512], F32, s0)
            kb.dma("sp", B_E, [], [B_E], lambda: nc.sync.dma_start(out=E_sb[:], in_=ebias[:, :]))
            kb.dma("sp", B_rb, [], [B_rb], lambda: nc.sync.dma_start(
                out=rb[:], in_=DV(rel_bias, 0, [[0, 128], [1, 512]])))
            def bslice(hd):
                return biasT[:, hd, :].rearrange("p (a b) -> p a b", a=2)
            for hd in range(16):
                kb.dma("sp", B_bias, [], [B_bias], lambda hd=hd: nc.sync.dma_start(
                    out=bslice(hd), in_=negmask[:, :].rearrange("p (a b) -> p a b", a=2)))
            for hd in range(16):
                for b in range(32):
                    kb.op("dve", [B_E, B_rb, B_bias], [B_bias], lambda hd=hd, b=b: nc.vector.scalar_tensor_tensor(
                        out=bslice(hd), in0=E_sb[:, b * 256:(b + 1) * 256].rearrange("p (a b) -> p a b", a=2),
                        scalar=rb[:, b * 16 + hd:b * 16 + hd + 1], in1=bslice(hd),
                        op0=ALU.mult, op1=ALU.add))
            kb.wait_all("sp", [B_E, B_rb])

        xb = [sb(f"xa{i}", [128, D], F32, pa) for i in range(2)]
        ss, B_ss = sb("ssa", [128, 4], F32, pa)
        junk, B_junk = sb("junka", [128, D], BF16, pa)
        xn, B_xn = sb("xna", [128, D], BF16, pa)
        xnT, B_xnT = sb("xnTa", [128, 16, 128], BF16, pa)
        uT, B_uT = sb("uTa", [128, 8, 128], BF16, pa)
        QTs = [sb(f"QTa{i}", [128, 8, 2, 128], BF16, pa) for i in range(2)]
        kT2 = [sb(f"kT2a{i}", [128, 4, 128], BF16, pa) for i in range(3)]
        Vg = [sb(f"Vga{i}", [128, 4, 65], BF16, pa) for i in range(3)]
        sc = [sb(f"sca{i}", [128, 512], F32, pa) for i in range(2)]
        pT = [sb(f"pTa{i}", [128, 512], BF16, pa) for i in range(2)]
        dens = [sb(f"dena{i}", [128, 4], F32, pa) for i in range(2)]
        B_psOp = [Buf("psOp0"), Buf("psOp1")]
        ya = [sb(f"yaa{i}", [128, 1024], BF16, pa) for i in range(2)]
        pst, B_pst = ps("pstA", [128, 16, 128], BF16, pa)
        psA, B_psA = ps("psA", [128, 1024], F32, pa)
        psB, B_psB = psA, B_psA
        psS = [ps(f"psS{i}", [128, 2, 256], F32, pa) for i in range(2)]
        B_pSp = [Buf("pSp0"), Buf("pSp1")]
        psO, B_psO = ps("psO", [128, 2, 2, 65], F32, pa)
        for i in range(3):
            kb.op("dve", [], [Vg[i][1]], lambda i=i: nc.vector.memset(Vg[i][0][:, :, 64:65], 1.0))
        for i in range(2):
            kb.op("dve", [], [QTs[i][1]], lambda i=i: nc.vector.memset(QTs[i][0][:], 0.0))

        kb.dma("sp", xb[0][1], [], [xb[0][1]], lambda: nc.sync.dma_start(out=xb[0][0][:], in_=x[0:128, :]))

        def first_half(i):
            xt, B_x = xb[i % 2]
            QT, B_QT = QTs[i % 2]
            kt, B_kt = kT2[i % 3]
            vg, B_vg = Vg[i % 3]
            if i + 1 < NT:
                nt_, B_n = xb[(i + 1) % 2]
                kb.dma("sp", B_n, [], [B_n], lambda: nc.sync.dma_start(
                    out=nt_[:], in_=x[(i + 1) * 128:(i + 2) * 128, :]))
            rmsnorm_tile(xt[:], B_x, g_mix, B_gmix, [(xn[:], B_xn)], ss, B_ss, junk, B_junk)
            yield
            transposes(xn, B_xn, 16, pst, B_pst, xnT, B_xnT)
            yield
            for half in range(2):
                kb.op("pe", [B_win, B_xnT], [B_psA], lambda half=half: [
                    nc.tensor.matmul(psA[:, c * 128:(c + 1) * 128], lhsT=w_in_sb[:, kc, c * 128:(c + 1) * 128],
                                     rhs=xnT[:, kc, :], start=(kc == 0), stop=(kc == 15))
                    for c in range(4 * half, 4 * half + 4) for kc in range(16)])
                yield
            kb.op("dve", [B_psA], [B_uT], lambda: nc.vector.tensor_copy(
                out=uT[:].rearrange("p a b -> p (a b)"), in_=psA[:]))
            kb.dma("sp", B_uT, [B_uT], [B_u], lambda: nc.sync.dma_start(
                out=DV(u_scr, i * 128, [[L, 128], [128 * L, 8], [1, 128]]), in_=uT[:]))
            yield
            for half in range(2):
                kb.op("pe", [B_win, B_xnT], [B_psA], lambda half=half: [
                    nc.tensor.matmul(psA[:, c * 128:(c + 1) * 128],
                                     lhsT=w_in_sb[:, kc, 1024 + c * 128:1024 + (c + 1) * 128],
                                     rhs=xnT[:, kc, :], start=(kc == 0), stop=(kc == 15))
                    for c in range(4 * half, 4 * half + 4) for kc in range(16)])
                yield
            for hf in range(2):
                kb.op("act", [B_psA], [B_QT], lambda hf=hf: nc.scalar.copy(
                    out=QT[hf * 64:(hf + 1) * 64, :, hf, :],
                    in_=psA[hf * 64:(hf + 1) * 64, :].rearrange("p (a b) -> p a b", a=8)))
            yield
            kb.op("pe", [B_win, B_xnT], [B_psA], lambda: [
                nc.tensor.matmul(psA[:, c * 128:(c + 1) * 128], lhsT=w_in_sb[:, kc, 2048 + c * 128:2048 + (c + 1) * 128],
                                 rhs=xnT[:, kc, :], start=(kc == 0), stop=(kc == 15))
                for c in range(4) for kc in range(16)] + [
                nc.tensor.matmul(psA[:, 512:768], lhsT=xnT[:, kc, :], rhs=w_in_sb[:, kc, 2560:2816],
                                 start=(kc == 0), stop=(kc == 15)) for kc in range(16)])
            kb.op("act", [B_psA], [B_kt], lambda: nc.scalar.copy(
                out=kt[:].rearrange("p a b -> p (a b)"), in_=psA[:, 0:512]))
            kb.op("dve", [B_psA], [B_vg], lambda: nc.vector.tensor_copy(
                out=vg[:, :, 0:64], in_=psA[:, 512:768].rearrange("p (g d) -> p g d", g=4)))
            yield

        def swa(i, gen):
            QT, B_QT = QTs[i % 2]
            yat, B_yat = ya[i % 2]
            kbs = [1] if i == 0 else [0, 1]

            def kvb(kb_):
                return (i + kb_ - 1) % 3

            def emit_scores(gp):
                g = gp // 2
                par = gp % 2
                rd = [B_QT] + [kT2[kvb(kb_)][1] for kb_ in kbs]
                kb.op("pe", rd, [B_pSp[par]], lambda: [
                    nc.tensor.matmul(psS[hf][0][:, par, kb_ * 128:(kb_ + 1) * 128],
                                     lhsT=kT2[kvb(kb_)][0][:, g, :],
                                     rhs=QT[:, gp, hf, :], start=True, stop=True)
                    for kb_ in kbs for hf in range(2)])

            emit_scores(0)
            for gp in range(8):
                g = gp // 2
                par = gp % 2
                if gp + 1 < 8:
                    emit_scores(gp + 1)
                B_pS = B_pSp[par]
                sct, B_sct = sc[par]
                pt, B_pt = pT[par]
                B_pO = B_psOp[par]
                dn, B_dn = dens[par]
                c0 = 128 * kbs[0]
                for hf in range(2):
                    kb.op("dve", [B_pS, B_bias], [B_sct], lambda hf=hf: nc.vector.scalar_tensor_tensor(
                        out=sct[:, hf * 256 + c0:(hf + 1) * 256], in0=psS[hf][0][:, par, c0:256], scalar=0.125,
                        in1=biasT[:, 2 * gp + hf, c0:256], op0=ALU.mult, op1=ALU.add))
                if c0:
                    kb.op("act", [B_sct], [B_pt], lambda: [nc.scalar.activation(
                        out=pt[:, hf * 256 + 128:(hf + 1) * 256], in_=sct[:, hf * 256 + 128:(hf + 1) * 256], func=AF.Exp)
                        for hf in range(2)])
                else:
                    kb.op("act", [B_sct], [B_pt], lambda: nc.scalar.activation(
                        out=pt[:], in_=sct[:], func=AF.Exp))
                rd = [B_pt] + [Vg[kvb(kb_)][1] for kb_ in kbs]
                kb.op("pe", rd, [B_pO], lambda: [
                    nc.tensor.matmul(psO[:, gp % 2, hf, :],
                                     lhsT=pt[:, (hf * 2 + kb_) * 128:(hf * 2 + kb_ + 1) * 128],
                                     rhs=Vg[kvb(kb_)][0][:, g, :],
                                     start=(kb_ == kbs[0]), stop=(kb_ == kbs[-1]))
                    for hf in range(2) for kb_ in kbs])
                kb.op("dve", [B_pO, B_esink], [B_dn], lambda: nc.vector.tensor_tensor(
                    out=dn[:, 0:2], in0=psO[:, gp % 2, :, 64], in1=esink[:, 2 * gp:2 * gp + 2], op=ALU.add))
                kb.op("dve", [B_dn], [B_dn], lambda: nc.vector.reciprocal(out=dn[:, 2:4], in_=dn[:, 0:2]))
                for hf in range(2):
                    kb.op("dve", [B_pO, B_dn], [B_yat], lambda hf=hf: nc.vector.tensor_scalar(
                        out=yat[:, (2 * gp + hf) * 64:(2 * gp + hf + 1) * 64], in0=psO[:, gp % 2, hf, 0:64],
                        scalar1=dn[:, 2 + hf:3 + hf], scalar2=None, op0=ALU.mult))
                if gen is not None:
                    next(gen, None)
                    next(gen, None)
            if gen is not None:
                for _ in gen:
                    pass
            kb.dma("sp", B_yat, [B_yat], [B_ya], lambda: nc.sync.dma_start(
                out=ya_scr[i * 128:(i + 1) * 128, :], in_=yat[:]))

        for _ in first_half(0):
            pass
        for i in range(NT):
            swa(i, first_half(i + 1) if i + 1 < NT else None)
        allb = [B_win, B_gmix, B_bias, B_esink, B_ss, B_junk, B_xn, B_xnT, B_uT, B_pst, B_psA,
                B_psO] + B_pSp + B_psOp + [b for _, b in xb + kT2 + Vg + sc + pT + ya + dens + QTs]
        for en in ("pe", "dve", "act", "pool", "sp"):
            kb.wait_all(en, allb)


    TWO_PI = 2.0 * math.pi
    def phase_b(pb):
        BbT, B_BbT = sb("BbT", [128, 32, 2, 128], BF16, pb)
        CL, B_CL = sb("CL", [128, 32, 2, 128], BF16, pb)
        Ct, B_tab = sb("Ct", [128, 32, TC], F32, pb)
        St, _ = sb("St", [128, 32, TC], F32, pb)
        rtab, _ = sb("rtab", [128, 32, TC], F32, pb)
        lbr, _ = sb("lbr", [128, 32], F32, pb)
        lbi, _ = sb("lbi", [128, 32], F32, pb)
        Kr, _ = sb("s5Kr", [128, 32], F32, pb)
        Ki, _ = sb("s5Ki", [128, 32], F32, pb)
        Ctb, _ = sb("Ctb", [128, 32 * TC], BF16, pb)
        Stb, _ = sb("Stb", [128, 32 * TC], BF16, pb)
        Dg, B_Dg = sb("Dg", [128, 8, 128], BF16, pb)
        bg, B_bg = sb("bg", [128, 8], F32, pb)
        wg, B_wg = sb("wg", [128, 8, 1024], BF16, pb)
        load_w_bf16(wg, B_wg, w_glu, 8, 1024)
        if "D" in phases:
            convert_tables()
        kb.dma("sp", B_bg, [], [B_bg], lambda: nc.sync.dma_start(
            out=bg[:], in_=DV(b_glu, 0, [[1, 128], [128, 8]]), allow_slow_non_contiguous=True))
        with ExitStack() as s1:
            B_su = Buf("s5setup")
            cnt = [0]

            def st(shape, dt=F32):
                cnt[0] += 1
                return s1.enter_context(nc.sbuf_tensor(f"s5t{cnt[0]}", list(shape), dt))

            def dv(fn):
                kb.op("dve", [B_su], [B_su], fn)

            def ac(fn):
                kb.op("act", [B_su], [B_su], fn)

            def tt(o, a, b, op):
                dv(lambda: nc.vector.tensor_tensor(out=o, in0=a, in1=b, op=op))

            def ts(o, a, sc_, op):
                dv(lambda: nc.vector.tensor_single_scalar(out=o, in_=a, scalar=sc_, op=op))

            lr = st([128, 32]); li = st([128, 32]); ldt = st([128, 32])
            kb.dma("sp", B_su, [], [B_su], lambda: nc.sync.dma_start(
                out=lr[:], in_=DV(lam_re, 0, [[1, 128], [128, 32]]), allow_slow_non_contiguous=True))
            kb.dma("sp", B_su, [], [B_su], lambda: nc.sync.dma_start(
                out=li[:], in_=DV(lam_im, 0, [[1, 128], [128, 32]]), allow_slow_non_contiguous=True))
            for a in range(2):
                kb.dma("sp", B_su, [], [B_su], lambda a=a: nc.sync.dma_start(
                    out=ldt[a * 64:(a + 1) * 64, :], in_=DV(log_dt, a, [[0, 64], [2, 32]]),
                    allow_slow_non_contiguous=True))
            bre = st([128, 32, 16]); bim = st([128, 32, 16])
            kb.dma("sp", B_su, [], [B_su], lambda: nc.sync.dma_start(
                out=bre[:], in_=DV(b_re, 0, [[16, 128], [2048, 32], [1, 16]])))
            kb.dma("sp", B_su, [], [B_su], lambda: nc.sync.dma_start(
                out=bim[:], in_=DV(b_im, 0, [[16, 128], [2048, 32], [1, 16]])))
            Cd = [st([128, 8, 128]) for _ in range(2)]
            for ri, csrc in enumerate((c_re, c_im)):
                for dup in range(2):
                    kb.dma("sp", B_su, [], [B_su], lambda ri=ri, csrc=csrc, dup=dup: nc.sync.dma_start(
                        out=Cd[ri][:, :, dup * 64:(dup + 1) * 64], in_=DV(csrc, 0, [[64, 128], [8192, 8], [1, 64]])))
            Dt = st([128, 8])
            kb.dma("sp", B_su, [], [B_su], lambda: nc.sync.dma_start(
                out=Dt[:], in_=DV(ssm_d, 0, [[1, 128], [128, 8]]), allow_slow_non_contiguous=True))

            dtt = st([128, 32]); rho = st([128, 32]); th = st([128, 32]); mag = st([128, 32])
            ac(lambda: nc.scalar.activation(out=dtt[:], in_=ldt[:], func=AF.Exp))
            tt(rho[:], lr[:], dtt[:], ALU.mult)
            tt(th[:], li[:], dtt[:], ALU.mult)
            ac(lambda: nc.scalar.activation(out=mag[:], in_=rho[:], func=AF.Exp))

            def sin_of(x_ap, shift, o_ap):
                t = st([128, 32]); ki = st([128, 32], I32); kf = st([128, 32]); r = st([128, 32]); m = st([128, 32])
                xs = st([128, 32])
                ts(xs[:], x_ap, shift, ALU.add)
                ts(t[:], xs[:], 1.0 / TWO_PI, ALU.mult)
                dv(lambda: nc.vector.tensor_copy(out=ki[:], in_=t[:]))
                dv(lambda: nc.vector.tensor_copy(out=kf[:], in_=ki[:]))
                dv(lambda: nc.vector.scalar_tensor_tensor(out=r[:], in0=kf[:], scalar=-TWO_PI, in1=xs[:],
                                                          op0=ALU.mult, op1=ALU.add))
                ts(m[:], r[:], math.pi, ALU.is_gt)
                dv(lambda: nc.vector.scalar_tensor_tensor(out=r[:], in0=m[:], scalar=-TWO_PI, in1=r[:],
                                                          op0=ALU.mult, op1=ALU.add))
                ts(m[:], r[:], -math.pi, ALU.is_lt)
                dv(lambda: nc.vector.scalar_tensor_tensor(out=r[:], in0=m[:], scalar=TWO_PI, in1=r[:],
                                                          op0=ALU.mult, op1=ALU.add))
                ts(r[:], r[:], math.pi, ALU.min)
                ts(r[:], r[:], -math.pi, ALU.max)
                ac(lambda: nc.scalar.activation(out=o_ap, in_=r[:], func=AF.Sin))

            sn = st([128, 32]); cs = st([128, 32])
            sin_of(th[:], 0.0, sn[:])
            sin_of(th[:], math.pi / 2, cs[:])
            tt(lbr[:], mag[:], cs[:], ALU.mult)
            tt(lbi[:], mag[:], sn[:], ALU.mult)
            nr = st([128, 32]); dd = st([128, 32]); t1_ = st([128, 32]); t2_ = st([128, 32])
            cr = st([128, 32]); ci = st([128, 32])
            ts(nr[:], lbr[:], -1.0, ALU.add)
            tt(t1_[:], lr[:], lr[:], ALU.mult)
            tt(t2_[:], li[:], li[:], ALU.mult)
            tt(dd[:], t1_[:], t2_[:], ALU.add)
            dv(lambda: nc.vector.reciprocal(out=dd[:], in_=dd[:]))
            tt(t1_[:], nr[:], lr[:], ALU.mult)
            tt(t2_[:], lbi[:], li[:], ALU.mult)
            tt(cr[:], t1_[:], t2_[:], ALU.add)
            tt(cr[:], cr[:], dd[:], ALU.mult)
            tt(t1_[:], lbi[:], lr[:], ALU.mult)
            tt(t2_[:], nr[:], li[:], ALU.mult)
            tt(ci[:], t1_[:], t2_[:], ALU.subtract)
            tt(ci[:], ci[:], dd[:], ALU.mult)
            bbr = st([128, 32, 16]); bbi = st([128, 32, 16]); tb1 = st([128, 32, 16]); tb2 = st([128, 32, 16])
            crb = V(cr, 0, [[1, 32], [0, 16]]); cib = V(ci, 0, [[1, 32], [0, 16]])
            tt(tb1[:], bre[:], crb, ALU.mult)
            tt(tb2[:], bim[:], cib, ALU.mult)
            tt(bbr[:], tb1[:], tb2[:], ALU.subtract)
            tt(tb1[:], bim[:], crb, ALU.mult)
            tt(tb2[:], bre[:], cib, ALU.mult)
            tt(bbi[:], tb1[:], tb2[:], ALU.add)
            dv(lambda: nc.vector.memset(Ct[:, :, 0:1], 1.0))
            dv(lambda: nc.vector.memset(St[:, :, 0:1], 0.0))
            cn = st([128, 32]); sn2 = st([128, 32]); tn = st([128, 32])
            dv(lambda: nc.vector.tensor_copy(out=cn[:], in_=cs[:]))
            dv(lambda: nc.vector.tensor_copy(out=sn2[:], in_=sn[:]))
            ta = st([128, 32, TC]); tbb = st([128, 32, TC])
            n = 1
            while n < TC:
                cnb = V(cn, 0, [[1, 32], [0, n]]); snb = V(sn2, 0, [[1, 32], [0, n]])
                tt(ta[:, :, 0:n], Ct[:, :, 0:n], cnb, ALU.mult)
                tt(tbb[:, :, 0:n], St[:, :, 0:n], snb, ALU.mult)
                tt(Ct[:, :, n:2 * n], ta[:, :, 0:n], tbb[:, :, 0:n], ALU.subtract)
                tt(ta[:, :, 0:n], St[:, :, 0:n], cnb, ALU.mult)
                tt(tbb[:, :, 0:n], Ct[:, :, 0:n], snb, ALU.mult)
                tt(St[:, :, n:2 * n], ta[:, :, 0:n], tbb[:, :, 0:n], ALU.add)
                tt(tn[:], cn[:], sn2[:], ALU.mult)
                tt(t1_[:], cn[:], cn[:], ALU.mult)
                tt(t2_[:], sn2[:], sn2[:], ALU.mult)
                tt(cn[:], t1_[:], t2_[:], ALU.subtract)
                ts(sn2[:], tn[:], 2.0, ALU.mult)
                n *= 2
            tt(Kr[:], mag[:], cn[:], ALU.mult)
            tt(Ki[:], mag[:], sn2[:], ALU.mult)
            dv(lambda: nc.vector.tensor_copy(out=Ctb[:], in_=Ct[:].rearrange("p a b -> p (a b)")))
            dv(lambda: nc.vector.tensor_copy(out=Stb[:], in_=St[:].rearrange("p a b -> p (a b)")))
            dv(lambda: nc.vector.tensor_copy(out=rtab[:], in_=V(mag, 0, [[1, 32], [0, TC]])))
            dv(lambda: nc.vector.memset(rtab[:, :, 0:1], 0.0))
            for t in range(8):
                kb.op("dve", [B_su, B_idf], [B_su, B_Dg], lambda t=t: nc.vector.tensor_scalar(
                    out=Dg[:, t, :], in0=ident_f[:], scalar1=Dt[:, t:t + 1], scalar2=None, op0=ALU.mult))
            Z = st([128, 32, 128])
            psT1, B_psT1 = ps("psT1", [128, 4, 128], F32, s1)
            for ri, bb in enumerate((bbr, bbi)):
                dv(lambda: nc.vector.memset(Z[:], 0.0))
                for a in range(2):
                    dv(lambda a=a, bb=bb: nc.vector.tensor_copy(
                        out=V(Z, 16 * a, [[512, 8], [160, 4], [1, 16]], p0=64 * a, npart=64),
                        in_=V(bb, 0, [[64, 8], [16, 4], [1, 16]], p0=64 * a, npart=64)))
                for j4 in range(8):
                    kb.op("pe", [B_su, B_idf], [B_psT1], lambda j4=j4: [
                        nc.tensor.transpose(out=psT1[:, jj, :], in_=Z[:, 4 * j4 + jj, :], identity=ident_f[:])
                        for jj in range(4)])
                    kb.op("act", [B_psT1], [B_BbT], lambda j4=j4, ri=ri: nc.scalar.copy(
                        out=BbT[:, 4 * j4:4 * j4 + 4, ri, :], in_=psT1[:]))
            kb.op("dve", [], [B_CL], lambda: nc.vector.memset(CL[:], 0.0))
            for ri in range(2):
                for t in range(8):
                    kb.op("pe", [B_su, B_idf], [B_psT1], lambda t=t, ri=ri: nc.tensor.transpose(
                        out=psT1[:, 0, :], in_=Cd[ri][:, t, :], identity=ident_f[:]))
                    for a in range(2):
                        kb.op("dve", [B_psT1], [B_CL], lambda t=t, ri=ri, a=a: nc.vector.tensor_scalar(
                            out=V(CL, (4 * t) * 256 + ri * 128 + 16 * a, [[288, 4], [1, 16]], p0=64 * a, npart=64),
                            in0=V(psT1, 16 * a, [[32, 4], [1, 16]], p0=64 * a, npart=64),
                            scalar1=(1.0 if ri == 0 else -1.0), scalar2=None, op0=ALU.mult))
            kb.op("dve", [B_su], [B_tab], lambda: nc.vector.tensor_copy(out=lbr[:], in_=lbr[:]))
            for en in ("pe", "dve", "act", "sp"):
                kb.wait_all(en, [B_su, B_psT1])

        NBLK = L // 512
        ub = [sb(f"ub{i}", [128, 8, 512], BF16, pb) for i in range(2)]
        tA, B_tA = sb("s5tA", [128, 1024], F32, pb)
        tB, B_tB = sb("s5tB", [128, 1024], F32, pb)
        tC, B_tC = sb("s5tC", [128, 1024], BF16, pb)
        tD, B_tD = sb("s5tD", [128, 1024], BF16, pb)
        zrb, B_zrb = sb("s5zrb", [128, 1024], BF16, pb)
        zib, B_zib = sb("s5zib", [128, 1024], BF16, pb)
        zr, B_zr = sb("s5zr", [128, 32, TC], F32, pb)
        zi, B_zi = sb("s5zi", [128, 32, TC], F32, pb)
        zrs2 = [sb(f"s5zrs{i}", [128, 1024], F32, pb) for i in range(2)]
        zis2 = [sb(f"s5zis{i}", [128, 1024], F32, pb) for i in range(2)]
        XRb, B_XRb = sb("s5XRb", [128, 32, TC], BF16, pb)
        XIb, B_XIb = sb("s5XIb", [128, 32, TC], BF16, pb)
        m1, B_m = sb("s5m1", [128, 32], F32, pb)
        m2, _ = sb("s5m2", [128, 32], F32, pb)
        yg = [sb(f"s5yg{i}", [128, 8, 512], BF16, pb) for i in range(2)]
        sig, B_sig = sb("s5sig", [128, 512], F32, pb)
        ysb, B_ysb = sb("s5ysb", [128, 8, 512], BF16, pb)
        psBU = [ps(f"psBU{i}", [128, 1024], F32, pb) for i in range(2)]
        psY, B_psY = ps("psY", [128, 8, TC], F32, pb)
        psZ = [ps(f"psZ{i}", [128, 512], F32, pb) for i in range(2)]
        Ctf = Ct[:].rearrange("p a b -> p (a b)")
        Stf = St[:].rearrange("p a b -> p (a b)")
        rtf = rtab[:].rearrange("p a b -> p (a b)")

        def vtt(o, B_o, a, B_a, b, B_b, op, eng="dve"):
            h = nc.vector if eng == "dve" else nc.gpsimd
            kb.op(eng, [B_a, B_b], [B_o], lambda: h.tensor_tensor(out=o, in0=a, in1=b, op=op))

        def load_ub(blk):
            t_, B_ = ub[blk % 2]
            kb.dma("sp", B_, [B_u], [B_], lambda: nc.sync.dma_start(
                out=t_[:], in_=DV(u_scr, blk * 512, [[L, 128], [128 * L, 8], [1, 512]])))

        load_ub(0)
        gchunk = 0
        for blk in range(NBLK):
            if blk + 1 < NBLK:
                load_ub(blk + 1)
            ubt, B_ub = ub[blk % 2]
            ygt, B_yg = yg[blk % 2]
            for ci in range(512 // TC):
                s0 = ci * TC
                par = gchunk % 2
                zrs_p, B_zrsp = zrs2[1 - par]; zis_p, B_zisp = zis2[1 - par]
                zrs, B_zrs = zrs2[par]; zis, B_zis = zis2[par]
                for ri in range(2):
                    kb.op("pe", [B_BbT, B_ub], [psBU[ri][1]], lambda ri=ri, s0=s0, ubt=ubt: [
                        nc.tensor.matmul(psBU[ri][0][:, j * TC:(j + 1) * TC], lhsT=BbT[:, j, ri, :],
                                         rhs=ubt[:, j // 4, s0:s0 + TC], start=True, stop=True)
                        for j in range(32)])
                bur, B_bur = psBU[0]; bui, B_bui = psBU[1]
                zrf = zr[:].rearrange("p a b -> p (a b)"); zif = zi[:].rearrange("p a b -> p (a b)")
                vtt(tA[:], B_tA, bur[:], B_bur, Ctf, B_tab, ALU.mult)
                vtt(tB[:], B_tB, bui[:], B_bui, Stf, B_tab, ALU.mult)
                vtt(zrf, B_zr, tA[:], B_tA, tB[:], B_tB, ALU.add)
                vtt(tA[:], B_tA, bui[:], B_bui, Ctf, B_tab, ALU.mult)
                vtt(tB[:], B_tB, bur[:], B_bur, Stf, B_tab, ALU.mult)
                vtt(zif, B_zi, tA[:], B_tA, tB[:], B_tB, ALU.subtract)
                if gchunk > 0:
                    xr1 = V(zrs_p, TC - 1, [[TC, 32]]); xi1 = V(zis_p, TC - 1, [[TC, 32]])
                    B_xrp, B_xip = B_zrsp, B_zisp
                    vtt(m1[:], B_m, Kr[:], B_tab, xr1, B_xrp, ALU.mult)
                    vtt(m2[:], B_m, Ki[:], B_tab, xi1, B_xip, ALU.mult)
                    vtt(m1[:], B_m, m1[:], B_m, m2[:], B_m, ALU.subtract)
                    vtt(zr[:, :, 0], B_zr, zr[:, :, 0], B_zr, m1[:], B_m, ALU.add)
                    vtt(m1[:], B_m, Kr[:], B_tab, xi1, B_xip, ALU.mult)
                    vtt(m2[:], B_m, Ki[:], B_tab, xr1, B_xrp, ALU.mult)
                    vtt(m1[:], B_m, m1[:], B_m, m2[:], B_m, ALU.add)
                    vtt(zi[:, :, 0], B_zi, zi[:, :, 0], B_zi, m1[:], B_m, ALU.add)
                kb.op("dve", [B_tab, B_zr], [B_zrs], lambda zrf=zrf: nc.vector.tensor_tensor_scan(
                    out=zrs[:], data0=rtf, data1=zrf, initial=0.0, op0=ALU.mult, op1=ALU.add))
                kb.op("dve", [B_tab, B_zi], [B_zis], lambda zif=zif: nc.vector.tensor_tensor_scan(
                    out=zis[:], data0=rtf, data1=zif, initial=0.0, op0=ALU.mult, op1=ALU.add))
                kb.op("act", [B_zrs], [B_zrb], lambda zrs=zrs: nc.scalar.copy(out=zrb[:], in_=zrs[:]))
                kb.op("act", [B_zis], [B_zib], lambda zis=zis: nc.scalar.copy(out=zib[:], in_=zis[:]))
                xrf = XRb[:].rearrange("p a b -> p (a b)"); xif = XIb[:].rearrange("p a b -> p (a b)")
                vtt(tC[:], B_tC, zrb[:], B_zrb, Ctb[:], B_tab, ALU.mult)
                vtt(tD[:], B_tD, zib[:], B_zib, Stb[:], B_tab, ALU.mult)
                vtt(xrf, B_XRb, tC[:], B_tC, tD[:], B_tD, ALU.subtract)
                vtt(tC[:], B_tC, zrb[:], B_zrb, Stb[:], B_tab, ALU.mult)
                vtt(tD[:], B_tD, zib[:], B_zib, Ctb[:], B_tab, ALU.mult)
                vtt(xif, B_XIb, tC[:], B_tC, tD[:], B_tD, ALU.add)
                kb.op("pe", [B_CL, B_XRb, B_XIb, B_Dg, B_ub], [B_psY], lambda s0=s0, ubt=ubt: [
                    mm for t in range(8) for mm in (
                        [nc.tensor.matmul(psY[:, t, :], lhsT=CL[:, 4 * t + jl, ri, :],
                                          rhs=(XRb, XIb)[ri][:, 4 * t + jl, :],
                                          start=(jl == 0 and ri == 0), stop=False)
                         for jl in range(4) for ri in range(2)] +
                        [nc.tensor.matmul(psY[:, t, :], lhsT=Dg[:, t, :], rhs=ubt[:, t, s0:s0 + TC],
                                          start=False, stop=True)])])
                kb.op("act", [B_psY], [B_yg], lambda ygt=ygt, s0=s0: nc.scalar.activation(
                    out=ygt[:, :, s0:s0 + TC], in_=psY[:], func=AF.Gelu_apprx_tanh))
                gchunk += 1
            for co in range(8):
                pz, B_pz = psZ[co % 2]
                kb.op("pe", [B_wg, B_yg], [B_pz], lambda co=co, pz=pz, ygt=ygt: [
                    nc.tensor.matmul(pz[:], lhsT=wg[:, kc, co * 128:(co + 1) * 128], rhs=ygt[:, kc, :],
                                     start=(kc == 0), stop=(kc == 7)) for kc in range(8)])
                kb.op("act", [B_pz, B_bg], [B_sig], lambda co=co, pz=pz: nc.scalar.activation(
                    out=sig[:], in_=pz[:], func=AF.Sigmoid, bias=bg[:, co:co + 1]))
                kb.op("dve", [B_sig, B_yg], [B_ysb], lambda co=co, ygt=ygt: nc.vector.tensor_tensor(
                    out=ysb[:, co, :], in0=ygt[:, co, :], in1=sig[:], op=ALU.mult))
            kb.dma("sp", B_ysb, [B_ysb], [B_ys], lambda blk=blk: nc.sync.dma_start(
                out=DV(ys_scr, blk * 512, [[L, 128], [128 * L, 8], [1, 512]]), in_=ysb[:]))
        allb = [B_BbT, B_CL, B_tab, B_Dg, B_bg, B_wg, B_tA, B_tB, B_tC, B_tD, B_zr, B_zi, B_XRb, B_XIb, B_m,
                B_sig, B_ysb, B_psY, B_zrb, B_zib] + [b for _, b in ub + yg + psBU + psZ + zrs2 + zis2]
        for en in ("pe", "dve", "act", "pool", "sp"):
            kb.wait_all(en, allb)


    def phase_c(pc):
        wo, B_wo = sb("wo", [128, 16, D], BF16, pc)
        load_w_bf16(wo, B_wo, w_out, 16, D)
        wcq, B_wcq = sb("wcq", [128, 16, 512], BF16, pc)
        load_w_bf16(wcq, B_wcq, w_cq, 16, 512)
        wco, B_wco = sb("wco", [128, 4, D], BF16, pc)
        load_w_bf16(wco, B_wco, w_co, 4, D)
        g_cr, B_gcr = sb("g_cr", [128, D], F32, pc)
        bcast_load("sp", g_cr, B_gcr, norm_cross, D)
        KTm, B_KTm = sb("KTm", [128, 4, 256], BF16, pc)
        Vm, B_Vm = sb("Vm", [128, 2, 4, 129], BF16, pc)
        ss, B_ss = sb("ssc", [128, 4], F32, pc)
        junk, B_junk = sb("junkc", [128, D], BF16, pc)
        pst, B_pst = ps("pstC", [128, 16, 128], BF16, pc)
        psH, B_psH = ps("psH", [128, D], F32, pc)
        psQ, B_psQ = ps("psQ", [128, 512], F32, pc)
        psSc, B_psSc = ps("psSc", [128, 512], F32, pc)
        with ExitStack() as s2:
            wkv, B_wkv = sb("wkv", [128, 16, 1024], BF16, s2)
            load_w_bf16(wkv, B_wkv, w_ckv, 16, 1024)
            g_me, B_gme = sb("g_me", [128, D], F32, s2)
            bcast_load("sp", g_me, B_gme, norm_mem, D)
            mt_, B_mt = sb("memt", [128, D], F32, s2)
            mn, B_mn = sb("memn", [128, D], BF16, s2)
            memT, B_memT = sb("memT", [128, 16, 256], BF16, s2)
            for mt in range(2):
                kb.dma("sp", B_mt, [], [B_mt], lambda mt=mt: nc.sync.dma_start(
                    out=mt_[:], in_=mem[mt * 128:(mt + 1) * 128, :]))
                rmsnorm_tile(mt_[:], B_mt, g_me, B_gme, [(mn[:], B_mn)], ss, B_ss, junk, B_junk)
                transposes(mn, B_mn, 16, pst, B_pst, memT[:, :, mt * 128:(mt + 1) * 128], B_memT)
            for h in range(4):
                kb.op("pe", [B_wkv, B_memT], [B_psQ], lambda h=h: [
                    nc.tensor.matmul(psQ[:, 0:256], lhsT=wkv[:, kc, h * 128:(h + 1) * 128], rhs=memT[:, kc, :],
                                     start=(kc == 0), stop=(kc == 15)) for kc in range(16)])
                kb.op("act", [B_psQ], [B_KTm], lambda h=h: nc.scalar.copy(out=KTm[:, h, :], in_=psQ[:, 0:256]))
            kb.op("dve", [], [B_Vm], lambda: nc.vector.memset(Vm[:, :, :, 128:129], 1.0))
            for mt in range(2):
                kb.op("pe", [B_wkv, B_memT], [B_psSc], lambda mt=mt: [
                    nc.tensor.matmul(psSc[:], lhsT=memT[:, kc, mt * 128:(mt + 1) * 128], rhs=wkv[:, kc, 512:1024],
                                     start=(kc == 0), stop=(kc == 15)) for kc in range(16)])
                kb.op("act", [B_psSc], [B_Vm], lambda mt=mt: nc.scalar.copy(
                    out=Vm[:, mt, :, 0:128], in_=psSc[:].rearrange("p (h d) -> p h d", h=4)))
            for en in ("pe", "dve", "act", "pool", "sp"):
                kb.wait_all(en, [B_wkv, B_gme, B_mt, B_mn, B_memT])

        xc = [sb(f"xc{i}", [128, D], F32, pc) for i in range(2)]
        yac = [sb(f"yac{i}", [128, 1024], BF16, pc) for i in range(2)]
        ysc = [sb(f"ysc{i}", [128, 8, 128], BF16, pc) for i in range(2)]
        yaT, B_yaT = sb("yaT", [128, 8, 128], BF16, pc)
        h1, B_h1 = sb("h1c", [128, D], F32, pc)
        hn, B_hn = sb("hnc", [128, D], BF16, pc)
        hnT, B_hnT = sb("hnTc", [128, 16, 128], BF16, pc)
        qTc, B_qTc = sb("qTc", [128, 4, 128], BF16, pc)
        pTc, B_pTc = sb("pTc", [128, 512], BF16, pc)
        rc, B_rc = sb("rcc", [128, 2], F32, pc)
        on, B_on = sb("onc", [128, 512], BF16, pc)
        onT, B_onT = sb("onTc", [128, 4, 128], BF16, pc)
        h2t, B_h2t = sb("h2c", [128, D], F32, pc)

        def load_c(i):
            xt_, B_x_ = xc[i % 2]; ya_, B_ya_ = yac[i % 2]; ys_, B_ys_ = ysc[i % 2]
            kb.dma("sp", B_x_, [], [B_x_], lambda: nc.sync.dma_start(out=xt_[:], in_=x[i * 128:(i + 1) * 128, :]))
            kb.dma("sp", B_ya_, [B_ya], [B_ya_], lambda: nc.sync.dma_start(
                out=ya_[:], in_=ya_scr[i * 128:(i + 1) * 128, :]))
            kb.dma("sp", B_ys_, [B_ys], [B_ys_], lambda: nc.sync.dma_start(
                out=ys_[:], in_=DV(ys_scr, i * 128, [[L, 128], [128 * L, 8], [1, 128]])))

        load_c(0)
        for i in range(NT):
            if i + 1 < NT:
                load_c(i + 1)
            xt, B_x = xc[i % 2]; yat, B_yat = yac[i % 2]; yst, B_yst = ysc[i % 2]
            transposes(yat, B_yat, 8, pst, B_pst, yaT, B_yaT)
            kb.op("pe", [B_wo, B_yst, B_yaT], [B_psH], lambda yst=yst: [
                nc.tensor.matmul(psH[:, nb * 512:(nb + 1) * 512],
                                 lhsT=(yst[:, kc, :] if kc < 8 else yaT[:, kc - 8, :]),
                                 rhs=wo[:, kc, nb * 512:(nb + 1) * 512], start=(kc == 0), stop=(kc == 15))
                for nb in range(4) for kc in range(16)])
            kb.op("dve", [B_psH, B_x], [B_h1], lambda xt=xt: nc.vector.tensor_tensor(
                out=h1[:], in0=psH[:], in1=xt[:], op=ALU.add))
            rmsnorm_tile(h1[:], B_h1, g_cr, B_gcr, [(hn[:], B_hn)], ss, B_ss, junk, B_junk)
            transposes(hn, B_hn, 16, pst, B_pst, hnT, B_hnT)
            kb.op("pe", [B_wcq, B_hnT], [B_psQ], lambda: [
                nc.tensor.matmul(psQ[:, h * 128:(h + 1) * 128], lhsT=wcq[:, kc, h * 128:(h + 1) * 128],
                                 rhs=hnT[:, kc, :], start=(kc == 0), stop=(kc == 15))
                for h in range(4) for kc in range(16)])
            kb.op("act", [B_psQ], [B_qTc], lambda: nc.scalar.copy(
                out=qTc[:].rearrange("p a b -> p (a b)"), in_=psQ[:]))
            for hh in range(2):
                kb.op("pe", [B_KTm, B_qTc], [B_psSc], lambda hh=hh: [
                    nc.tensor.matmul(psSc[:, (mt * 2 + hl) * 128:(mt * 2 + hl + 1) * 128],
                                     lhsT=KTm[:, 2 * hh + hl, mt * 128:(mt + 1) * 128], rhs=qTc[:, 2 * hh + hl, :],
                                     start=True, stop=True) for mt in range(2) for hl in range(2)])
                kb.op("act", [B_psSc], [B_pTc], lambda: nc.scalar.activation(
                    out=pTc[:], in_=psSc[:], func=AF.Exp, scale=128.0 ** -0.5))
                kb.op("pe", [B_pTc, B_Vm], [B_psQ], lambda hh=hh: [
                    nc.tensor.matmul(psQ[:, hl * 256:hl * 256 + 129],
                                     lhsT=pTc[:, (mt * 2 + hl) * 128:(mt * 2 + hl + 1) * 128],
                                     rhs=Vm[:, mt, 2 * hh + hl, :], start=(mt == 0), stop=(mt == 1))
                    for hl in range(2) for mt in range(2)])
                kb.op("dve", [B_psQ], [B_rc], lambda: nc.vector.reciprocal(
                    out=rc[:], in_=V(psQ, 128, [[256, 2]])))
                for hl in range(2):
                    kb.op("dve", [B_psQ, B_rc], [B_on], lambda hh=hh, hl=hl: nc.vector.tensor_scalar(
                        out=on[:, (2 * hh + hl) * 128:(2 * hh + hl + 1) * 128], in0=psQ[:, hl * 256:hl * 256 + 128],
                        scalar1=rc[:, hl:hl + 1], scalar2=None, op0=ALU.mult))
            transposes(on, B_on, 4, pst, B_pst, onT, B_onT)
            kb.op("pe", [B_wco, B_onT], [B_psH], lambda: [
                nc.tensor.matmul(psH[:, nb * 512:(nb + 1) * 512], lhsT=onT[:, kc, :],
                                 rhs=wco[:, kc, nb * 512:(nb + 1) * 512], start=(kc == 0), stop=(kc == 3))
                for nb in range(4) for kc in range(4)])
            kb.op("dve", [B_psH, B_h1], [B_h2t], lambda: nc.vector.tensor_tensor(
                out=h2t[:], in0=psH[:], in1=h1[:], op=ALU.add))
            kb.dma("sp", B_h2t, [B_h2t], [B_h2], lambda i=i: nc.sync.dma_start(
                out=h2_scr[i * 128:(i + 1) * 128, :], in_=h2t[:]))
        allb = [B_wo, B_wcq, B_wco, B_gcr, B_KTm, B_Vm, B_ss, B_junk, B_pst, B_psH, B_psQ, B_psSc, B_yaT, B_h1,
                B_hn, B_hnT, B_qTc, B_pTc, B_rc, B_on, B_onT, B_h2t] + [b for _, b in xc + yac + ysc]
        for en in ("pe", "dve", "act", "pool", "sp"):
            kb.wait_all(en, allb)


    import os
    NG = int(os.environ.get('PEER_NG', '10'))
    SKIP_DVE = os.environ.get('PEER_SKIP_DVE') == '1'
    SKIP_DMA = os.environ.get('PEER_SKIP_DMA') == '1'
    def phase_d(pd):
        wpq, B_wpq = sb("wpq", [128, 16, D], BF16, pd)
        load_w_bf16(wpq, B_wpq, w_pq, 16, D)
        skT, B_skT = sb("skT", [128, 16, 128], BF16, pd)
        g_ff, B_gff = sb("g_ff", [128, D], F32, pd)
        bcast_load("sp", g_ff, B_gff, norm_ffn, D)
        g_fi, B_gfi = sb("g_fi", [128, D], F32, pd)
        bcast_load("sp", g_fi, B_gfi, norm_final, D)
        io16, B_io = sb("io16", [128, 16], F32, pd)
        bcast_load("sp", io16, B_io, iota16, 16)
        ss, B_ss = sb("ssd", [128, 4], F32, pd)
        junk, B_junk = sb("junkd", [128, D], BF16, pd)
        pst, B_pst = ps("pstD", [128, 16, 128], BF16, pd)
        psG, B_psG = ps("psG", [128, D], F32, pd)
        W = [sb(f"Wd{i}", [128, D], F32, pd) for i in range(2)]
        W = [W[0], W[1], W[0], W[1]]
        if "B" not in phases:
            convert_tables()
        kb.wait_all("pool", cvb)
        kb.dma("sp", W[0][1], [], [W[0][1]], lambda: nc.sync.dma_start(
            out=W[0][0][:].rearrange("p (a b) -> p a b", a=16), in_=DV(sub_keys, 0, [[128, 128], [16384, 16], [1, 128]])))
        for q4 in range(4):
            kb.op("pe", [W[0][1], B_idf], [B_psG], lambda q4=q4: [
                nc.tensor.transpose(out=psG[:, (4 * q4 + jj) * 128:(4 * q4 + jj + 1) * 128],
                                    in_=W[0][0][:, (4 * q4 + jj) * 128:(4 * q4 + jj + 1) * 128], identity=ident_f[:])
                for jj in range(4)])
        kb.op("act", [B_psG], [B_skT], lambda: nc.scalar.copy(out=skT[:].rearrange("p a b -> p (a b)"), in_=psG[:]))

        h2ts = [sb("h2d0", [128, D], F32, pd)]
        idxs = [sb(f"idxd{i}", [128, 128], I32, pd) for i in range(2)]
        ss2, B_ss2 = sb("ssd2", [128, 4], F32, pd)
        junk2, B_junk2 = sb("junkd2", [128, D], BF16, pd)
        psF, B_psF = ps("psF", [128, 1024], F32, pd)
        hnf, B_hnf = sb("hnfd", [128, D], F32, pd)
        hnb, B_hnb = sb("hnbd", [128, D], BF16, pd)
        hnT, B_hnT = sb("hnTd", [128, 16, 128], BF16, pd)
        qTp, B_qTp = sb("qTpd", [128, 16, 128], BF16, pd)
        stp, B_stp = sb("stopd", [128, 16, 16], F32, pd)
        itp, B_itp = sb("itopd", [128, 16, 16], U32, pd)
        itf, B_itf = sb("itfd", [128, 16, 16], F32, pd)
        best, B_best = sb("bestd", [128, 8, 16], F32, pd)
        pos, B_pos = sb("posd", [128, 8, 16], U32, pd)
        ai, B_ai = sb("aid", [128, 128], U32, pd)
        af, B_af = sb("afd", [128, 128], F32, pd)
        bf_, B_bf = sb("bfd", [128, 128], F32, pd)
        i1g, B_i1g = sb("i1gd", [128, 128], F32, pd)
        i2g, B_i2g = sb("i2gd", [128, 128], F32, pd)
        ebs = [sb(f"ebd{i}", [128, 8, 16], F32, pd) for i in range(2)]
        sm, B_sm = sb("smd", [128, 16], F32, pd)
        actv, B_actv = sb("actd", [128, 128], F32, pd)
        coef, B_coef = sb("coefd", [128, 128], F32, pd)
        acc, B_acc = sb("accd", [128, D], F32, pd)
        gb = [sb(f"gbd{i}", [128, D], BF16, pd) for i in range(NG)]
        dgs = [sb(f"dgd{i}", [128, 16, 128], BF16, pd) for i in range(2)]
        gcount = [0]
        (s_sb, B_s), (s2_, B_s2), (cand, B_cand), (c2, B_c2) = W

        def front(i):
            h2t, B_h2t = h2ts[0]
            idx, B_idx = idxs[i % 2]
            eb, B_eb = ebs[i % 2]
            kb.dma("sp", B_h2t, [B_h2], [B_h2t], lambda: nc.sync.dma_start(
                out=h2t[:], in_=h2_scr[i * 128:(i + 1) * 128, :]))
            rmsnorm_tile(h2t[:], B_h2t, g_ff, B_gff, [(hnf[:], B_hnf), (hnb[:], B_hnb)], ss, B_ss, junk2, B_junk2)
            yield
            transposes(hnb, B_hnb, 16, pst, B_pst, hnT, B_hnT)
            yield
            for half in range(2):
                kb.op("pe", [B_wpq, B_hnT], [B_psF], lambda half=half: [
                    nc.tensor.matmul(psF[:, hl * 128:(hl + 1) * 128],
                                     lhsT=wpq[:, kc, (8 * half + hl) * 128:(8 * half + hl + 1) * 128],
                                     rhs=hnT[:, kc, :], start=(kc == 0), stop=(kc == 15))
                    for hl in range(8) for kc in range(16)])
                kb.op("act", [B_psF], [B_qTp], lambda half=half: nc.scalar.copy(
                    out=qTp[:, 8 * half:8 * half + 8, :].rearrange("p a b -> p (a b)"), in_=psF[:]))
                yield
            for half in range(2):
                kb.op("pe", [B_qTp, B_skT], [B_psF], lambda half=half: [
                    nc.tensor.matmul(psF[:, hl * 128:(hl + 1) * 128], lhsT=qTp[:, 8 * half + hl, :],
                                     rhs=skT[:, 8 * half + hl, :], start=True, stop=True) for hl in range(8)])
                kb.op("act", [B_psF], [B_s], lambda half=half: nc.scalar.copy(
                    out=s_sb[:, half * 1024:(half + 1) * 1024], in_=psF[:]))
                yield
            for hc in range(16):
                sv = s_sb[:, hc * 128:(hc + 1) * 128]
                s2v = s2_[:, hc * 128:(hc + 1) * 128]
                kb.op("dve", [B_s], [B_stp], lambda hc=hc, sv=sv: nc.vector.max(out=stp[:, hc, 0:8], in_=sv))
                kb.op("dve", [B_s, B_stp], [B_itp], lambda hc=hc, sv=sv: nc.vector.max_index(
                    out=itp[:, hc, 0:8], in_max=stp[:, hc, 0:8], in_values=sv))
                kb.op("dve", [B_s, B_stp], [B_s2], lambda hc=hc, sv=sv, s2v=s2v: nc.vector.match_replace(
                    out=s2v, in_to_replace=stp[:, hc, 0:8], in_values=sv, imm_value=-1e30))
                kb.op("dve", [B_s2], [B_stp], lambda hc=hc, s2v=s2v: nc.vector.max(out=stp[:, hc, 8:16], in_=s2v))
                kb.op("dve", [B_s2, B_stp], [B_itp], lambda hc=hc, s2v=s2v: nc.vector.max_index(
                    out=itp[:, hc, 8:16], in_max=stp[:, hc, 8:16], in_values=s2v))
                yield
            kb.op("dve", [B_stp], [B_cand], lambda: nc.vector.tensor_tensor(
                out=cand[:].rearrange("p (h a b) -> p h a b", h=8, a=16),
                in0=V(stp, 0, [[32, 8], [1, 16], [0, 16]]), in1=V(stp, 16, [[32, 8], [0, 16], [1, 16]]), op=ALU.add))
            yield
            for h in range(8):
                cv = cand[:, h * 256:(h + 1) * 256]
                c2v = c2[:, h * 256:(h + 1) * 256]
                kb.op("dve", [B_cand], [B_best], lambda h=h, cv=cv: nc.vector.max(out=best[:, h, 0:8], in_=cv))
                kb.op("dve", [B_cand, B_best], [B_pos], lambda h=h, cv=cv: nc.vector.max_index(
                    out=pos[:, h, 0:8], in_max=best[:, h, 0:8], in_values=cv))
                kb.op("dve", [B_cand, B_best], [B_c2], lambda h=h, cv=cv, c2v=c2v: nc.vector.match_replace(
                    out=c2v, in_to_replace=best[:, h, 0:8], in_values=cv, imm_value=-1e30))
                kb.op("dve", [B_c2], [B_best], lambda h=h, c2v=c2v: nc.vector.max(out=best[:, h, 8:16], in_=c2v))
                kb.op("dve", [B_c2, B_best], [B_pos], lambda h=h, c2v=c2v: nc.vector.max_index(
                    out=pos[:, h, 8:16], in_max=best[:, h, 8:16], in_values=c2v))
                yield
            posf = pos[:].rearrange("p a b -> p (a b)")
            kb.op("dve", [B_itp], [B_itf], lambda: nc.vector.tensor_copy(out=itf[:], in_=itp[:]))
            kb.op("dve", [B_pos], [B_ai], lambda: nc.vector.tensor_single_scalar(
                out=ai[:], in_=posf, scalar=4, op=ALU.logical_shift_right))
            kb.op("dve", [B_ai], [B_af], lambda: nc.vector.tensor_copy(out=af[:], in_=ai[:]))
            kb.op("dve", [B_pos], [B_ai], lambda: nc.vector.tensor_single_scalar(
                out=ai[:], in_=posf, scalar=15, op=ALU.bitwise_and))
            kb.op("dve", [B_ai], [B_bf], lambda: nc.vector.tensor_copy(out=bf_[:], in_=ai[:]))
            yield
            for (sel, B_sel, c_off, og, B_og) in ((af, B_af, 0, i1g, B_i1g), (bf_, B_bf, 16, i2g, B_i2g)):
                kb.op("dve", [B_sel, B_io], [B_s], lambda sel=sel: nc.vector.tensor_tensor(
                    out=s_sb[:].rearrange("p (h k a) -> p h k a", h=8, k=16),
                    in0=V(sel, 0, [[16, 8], [1, 16], [0, 16]]), in1=V(io16, 0, [[0, 8], [0, 16], [1, 16]]),
                    op=ALU.is_equal))
                kb.op("dve", [B_s, B_itf], [B_s2], lambda c_off=c_off: nc.vector.tensor_tensor(
                    out=s2_[:].rearrange("p (h k a) -> p h k a", h=8, k=16),
                    in0=s_sb[:].rearrange("p (h k a) -> p h k a", h=8, k=16),
                    in1=V(itf, c_off, [[32, 8], [0, 16], [1, 16]]), op=ALU.mult))
                kb.op("dve", [B_s2], [B_og], lambda og=og: nc.vector.tensor_reduce(
                    out=og[:], in_=s2_[:].rearrange("p (n a) -> p n a", a=16), axis=AX.X, op=ALU.add))
                yield
            kb.op("dve", [B_i1g, B_i2g], [B_i1g], lambda: nc.vector.scalar_tensor_tensor(
                out=i1g[:], in0=i1g[:], scalar=128.0, in1=i2g[:], op0=ALU.mult, op1=ALU.add))
            kb.op("dve", [B_i1g], [B_idx], lambda: nc.vector.tensor_copy(out=idx[:], in_=i1g[:]))
            kb.op("dve", [B_best], [B_eb], lambda: nc.vector.tensor_tensor(
                out=eb[:], in0=best[:], in1=V(best, 0, [[16, 8], [0, 16]]), op=ALU.subtract))
            kb.op("act", [B_eb], [B_eb], lambda: nc.scalar.activation(out=eb[:], in_=eb[:], func=AF.Exp))
            kb.op("dve", [B_eb], [B_sm], lambda: nc.vector.tensor_reduce(
                out=sm[:, 0:8], in_=eb[:], axis=AX.X, op=ALU.add))
            kb.op("dve", [B_sm], [B_sm], lambda: nc.vector.reciprocal(out=sm[:, 8:16], in_=sm[:, 0:8]))
            kb.op("dve", [B_eb, B_sm], [B_eb], lambda: nc.vector.tensor_tensor(
                out=eb[:], in0=eb[:], in1=V(sm, 8, [[1, 8], [0, 16]]), op=ALU.mult))
            yield

        def useg(i):
            idx, B_idx = idxs[i % 2]
            eb, B_eb = ebs[i % 2]
            for hk in range(128):
                g_, B_g = gb[gcount[0] % NG]
                gcount[0] += 1
                kb.dma("pool", B_g, [B_idx], [B_g], lambda hk=hk, g_=g_: nc.gpsimd.indirect_dma_start(
                    out=g_[:], out_offset=None, in_=u_bf[:, :],
                    in_offset=bass.IndirectOffsetOnAxis(ap=idx[:, hk:hk + 1], axis=0)))
                kb.op("dve", [B_g, B_hnf], [B_junk, B_actv], lambda hk=hk, g_=g_: nc.vector.scalar_tensor_tensor(
                    out=junk[:], in0=g_[:], scalar=1.0, in1=hnf[:], op0=ALU.mult, op1=ALU.mult,
                    accum_out=actv[:, hk:hk + 1]))
            kb.op("act", [B_actv], [B_actv], lambda: nc.scalar.activation(
                out=actv[:], in_=actv[:], func=AF.Gelu_apprx_tanh))
            kb.op("dve", [B_actv, B_eb], [B_coef], lambda: nc.vector.tensor_tensor(
                out=coef[:], in0=actv[:], in1=eb[:].rearrange("p a b -> p (a b)"), op=ALU.mult))

        def vseg(i, gen):
            idx, B_idx = idxs[i % 2]
            kb.dma("sp", B_acc, [B_h2], [B_acc], lambda: nc.sync.dma_start(
                out=acc[:], in_=h2_scr[i * 128:(i + 1) * 128, :]))
            for hk in range(128):
                dg_, B_dg = dgs[(hk // 16) % 2]
                if hk % 16 == 0:
                    kb.op("dve", [B_coef, B_idf], [B_dg], lambda hk=hk, dg_=dg_: nc.vector.tensor_tensor(
                        out=dg_[:], in0=V(ident_f, 0, [[0, 16], [1, 128]]), in1=V(coef, hk, [[1, 16], [0, 128]]),
                        op=ALU.mult))
                    if gen is not None:
                        for _ in range(6):
                            next(gen, None)
                g_, B_g = gb[gcount[0] % NG]
                gcount[0] += 1
                kb.dma("pool", B_g, [B_idx], [B_g], lambda hk=hk, g_=g_: nc.gpsimd.indirect_dma_start(
                    out=g_[:], out_offset=None, in_=v_bf[:, :],
                    in_offset=bass.IndirectOffsetOnAxis(ap=idx[:, hk:hk + 1], axis=0)))
                kb.op("pe", [B_g, B_dg], [B_psG], lambda hk=hk, g_=g_, dg_=dg_: [
                    nc.tensor.matmul(psG[:, nb * 512:(nb + 1) * 512], lhsT=dg_[:, hk % 16, :],
                                     rhs=g_[:, nb * 512:(nb + 1) * 512], start=(hk == 0), stop=(hk == 127))
                    for nb in range(4)])
            if gen is not None:
                for _ in gen:
                    pass
            kb.op("dve", [B_psG, B_acc], [B_acc], lambda: nc.vector.tensor_tensor(
                out=acc[:], in0=psG[:], in1=acc[:], op=ALU.add))
            rmsnorm_tile(acc[:], B_acc, g_fi, B_gfi, [(acc[:], B_acc)], ss2, B_ss2, junk2, B_junk2)
            kb.dma("sp", B_acc, [B_acc], [B_out], lambda: nc.sync.dma_start(
                out=out[i * 128:(i + 1) * 128, :], in_=acc[:]))

        for _ in front(0):
            pass
        for i in range(NT):
            useg(i)
            vseg(i, front(i + 1) if i + 1 < NT else None)
        allb = [B_wpq, B_skT, B_gff, B_gfi, B_io, B_ss, B_ss2, B_junk, B_junk2, B_pst, B_psG, B_psF, B_hnf, B_hnb,
                B_hnT, B_qTp, B_stp, B_itp, B_itf, B_best, B_pos, B_ai, B_af, B_bf, B_i1g, B_i2g, B_sm, B_actv,
                B_coef, B_acc] + [b for _, b in W[:2] + gb + dgs + h2ts + idxs + ebs]
        for en in ("pe", "dve", "act", "pool", "sp"):
            kb.wait_all(en, allb)

    for nm, fn in (("A", phase_a), ("B", phase_b), ("C", phase_c), ("D", phase_d)):
        if nm in phases:
            with ExitStack() as pstack:
                fn(pstack)
    dbg["ya"] = ya_scr
    dbg["u"] = u_scr
    kb.wait_all("sp", [B_u, B_ya, B_ys, B_h2, B_out])
    es.close()
    return nc, dbg


def host_consts():
    W = 128
    qi = np.arange(W)[:, None]
    kj = np.arange(2 * W)[None, :]
    dist = qi + W - kj
    inwin = (dist >= 0) & (dist < W)
    dc = np.clip(dist, 0, W - 1)
    max_exact = 16
    d_f = np.maximum(dc, 1).astype(np.float32)
    large = max_exact + (np.log(d_f / max_exact) / math.log(128 / max_exact) * (32 - max_exact)).astype(np.int32)
    large = np.minimum(large, 31)
    bucket = np.where(dc < max_exact, dc, large)
    E = np.zeros((128, 32, 2, 128), np.float32)
    neg = np.zeros((128, 2, 128), np.float32)
    for kb_ in range(2):
        for k in range(128):
            kjj = kb_ * 128 + k
            for b in range(32):
                E[k, b, kb_, :] = ((bucket[:, kjj] == b) & inwin[:, kjj]).astype(np.float32)
            neg[k, kb_, :] = np.where(inwin[:, kjj], 0.0, -30000.0)
    return E.reshape(128, -1), neg.reshape(128, -1), np.eye(128, dtype=np.float32)


def make_in_maps(inp, NT=32, cores=NCORES):
    L = NT * 128
    E, neg, ident = host_consts()
    w_in = inp["w_in"][0]
    kcols = w_in[:, 2048:2304].reshape(D, 4, 1, 64)
    k2 = np.broadcast_to(kcols, (D, 4, 2, 64)).reshape(D, 512)
    w_in_l = np.ascontiguousarray(np.concatenate([w_in[:, :2048], k2, w_in[:, 2304:2560]], axis=1))
    shared = {
        "rel_bias": inp["rel_bias"], "norm_mix": inp["norm_mix"], "w_in": w_in_l,
        "lam_re": inp["ssm_lambda_re"][0], "lam_im": inp["ssm_lambda_im"][0],
        "b_re": inp["ssm_b_re"][0], "b_im": inp["ssm_b_im"][0],
        "c_re": inp["ssm_c_re"][0], "c_im": inp["ssm_c_im"][0],
        "ssm_d": inp["ssm_d"].reshape(1, 1024), "log_dt": inp["ssm_log_dt"],
        "w_glu": inp["ssm_w_glu"][0], "b_glu": inp["ssm_b_glu"], "sinks": inp["attn_sinks"],
        "w_out": inp["w_out"][0], "norm_cross": inp["norm_cross"], "norm_mem": inp["norm_mem"],
        "w_cq": inp["w_cq"][0], "w_ckv": inp["w_ckv"][0], "w_co": inp["w_co"][0],
        "norm_ffn": inp["norm_ffn"], "w_pq": inp["peer_w_q"][0],
        "sub_keys": inp["peer_sub_keys"][0].reshape(16, 128, 128),
        "peer_u": inp["peer_u"][0], "peer_v": inp["peer_v"][0],
        "norm_final": inp["norm_final"].reshape(1, D),
        "ebias": E, "negmask": neg, "ident": ident, "iota16": np.arange(16, dtype=np.float32).reshape(1, 16),
    }
    shared = {k: np.ascontiguousarray(np.asarray(v, dtype=np.float32)) for k, v in shared.items()}
    maps = []
    for c in range(cores):
        m = dict(shared)
        m["x"] = np.ascontiguousarray(np.asarray(inp["x"][c, :L], dtype=np.float32))
        m["mem"] = np.ascontiguousarray(np.asarray(inp["mem"][c], dtype=np.float32))
        maps.append(m)
    return maps


def kernel(**inputs):
    nc, _ = build(32)
    maps = make_in_maps(inputs, 32, NCORES)
    res = run_bass_kernel_spmd(nc, maps, core_ids=list(range(NCORES)))
    return np.stack([np.asarray(r["out"], dtype=np.float32) for r in res.results], axis=0)
```
